# Optimizing a Trainium2 kernel written in Bass

```python
import jax, jax.numpy as jnp
from jax import lax
import numpy as np

D_MODEL = 1024
BATCH = 8
SEQ = 2048
DEPTH = 4

N_MIXERS = 2
EXPAND = 2
D_INNER = EXPAND * D_MODEL
CHUNK = 128
A_GROUPS = 8
CONV_W = 3
EPS = 1e-6
N_A = (DEPTH + 1) // 2
N_B = DEPTH // 2

kernel_name = "hybrid_gmlp_shortconv_trunk"


def rmsnorm(x, g):
    xf = x.astype(jnp.float32)
    y = xf * lax.rsqrt(jnp.mean(xf * xf, axis=-1, keepdims=True) + EPS)
    return (y * g.astype(jnp.float32)).astype(x.dtype)


def mixer_a(h, w_in, v_norm_g, w_s, b_s, w_out):
    b, s, _ = h.shape
    u, v, z = jnp.split(h @ w_in, 3, axis=-1)
    v = rmsnorm(v, v_norm_g)
    vc = v.reshape(b, s // CHUNK, CHUNK, A_GROUPS, D_INNER // A_GROUPS)
    causal = jnp.tril(jnp.ones((CHUNK, CHUNK), dtype=bool))
    ws = jnp.where(causal[None], w_s, jnp.zeros((), w_s.dtype))
    mixed = jnp.einsum("gts,bnsgc->bntgc", ws, vc)
    mixed = mixed + jnp.transpose(b_s)[None, None, :, :, None]
    mixed = mixed.reshape(b, s, D_INNER)
    y = u * mixed * jax.nn.silu(z)
    return y @ w_out


def mixer_b(h, w_in, w_conv, w_out):
    s = h.shape[1]
    bg, cg, xs, z = jnp.split(h @ w_in, 4, axis=-1)
    xc = cg * xs
    xp = jnp.pad(xc, ((0, 0), (CONV_W - 1, 0), (0, 0)))
    conv = w_conv[0] * xp[:, 0:s, :]
    for k in range(1, CONV_W):
        conv = conv + w_conv[k] * xp[:, k:k + s, :]
    y = bg * conv * jax.nn.silu(z)
    return y @ w_out


def setup_inputs(seed: int = 0) -> dict:
    key = jax.random.key(seed)
    ks = jax.random.split(key, 12)
    f32 = jnp.float32
    x = jax.random.normal(ks[0], (BATCH, SEQ, D_MODEL), f32)
    norm_g = 1.0 + 0.02 * jax.random.normal(ks[1], (DEPTH, D_MODEL), f32)
    final_g = 1.0 + 0.02 * jax.random.normal(ks[2], (D_MODEL,), f32)
    a_w_in = jax.random.normal(ks[3], (N_A, D_MODEL, 3 * D_INNER), f32) * D_MODEL ** -0.5
    a_v_norm_g = 1.0 + 0.02 * jax.random.normal(ks[4], (N_A, D_INNER), f32)
    a_w_s = jax.random.normal(ks[5], (N_A, A_GROUPS, CHUNK, CHUNK), f32) * CHUNK ** -0.5
    a_b_s = 1.0 + 0.1 * jax.random.normal(ks[6], (N_A, A_GROUPS, CHUNK), f32)
    a_w_out = jax.random.normal(ks[7], (N_A, D_INNER, D_MODEL), f32) * D_INNER ** -0.5
    b_w_in = jax.random.normal(ks[8], (N_B, D_MODEL, 4 * D_INNER), f32) * D_MODEL ** -0.5
    b_w_conv = jax.random.normal(ks[9], (N_B, CONV_W, D_INNER), f32) * CONV_W ** -0.5
    b_w_out = jax.random.normal(ks[10], (N_B, D_INNER, D_MODEL), f32) * D_INNER ** -0.5
    return {"x": x, "norm_g": norm_g, "final_g": final_g,
            "a_w_in": a_w_in, "a_v_norm_g": a_v_norm_g, "a_w_s": a_w_s, "a_b_s": a_b_s,
            "a_w_out": a_w_out, "b_w_in": b_w_in, "b_w_conv": b_w_conv, "b_w_out": b_w_out}


def reference(x, norm_g, final_g, a_w_in, a_v_norm_g, a_w_s, a_b_s, a_w_out,
              b_w_in, b_w_conv, b_w_out):
    for i in range(DEPTH):
        h = rmsnorm(x, norm_g[i])
        j = i // N_MIXERS
        if i % N_MIXERS == 0:
            x = x + mixer_a(h, a_w_in[j], a_v_norm_g[j], a_w_s[j], a_b_s[j], a_w_out[j])
        else:
            x = x + mixer_b(h, b_w_in[j], b_w_conv[j], b_w_out[j])
    return rmsnorm(x, final_g)
```

```python
import numpy as np
from contextlib import ExitStack

import concourse.bass as bass
import concourse.mybir as mybir
from concourse.bass_utils import run_bass_kernel_spmd

F32 = mybir.dt.float32
BF16 = mybir.dt.bfloat16
AF = mybir.ActivationFunctionType
ALU = mybir.AluOpType
AX = mybir.AxisListType

D = 1024
DI = 2048
SEQ = 2048
T = 1024
NPASS = SEQ // T
TB = 512
NTB = T // TB
NTC = T // 128
KD = D // 128
KI = DI // 128
EPS = 1e-6
N_CORES = 8

C_GN = 0
C_GF = C_GN + 32
C_GV = C_GF + 8
C_WC = C_GV + 32
C_ID = C_WC + 96
C_MK = C_ID + 128
C_WS = C_MK + 128
C_BB = C_WS + 2048
NCST = C_BB + 2048

NSLOT = 4
TILES_A = 16
TILES_B = 20


class Op:
    __slots__ = ("eng", "fn", "reads", "writes", "dsem", "deps", "signal", "value")

    def __init__(self, eng, fn, reads, writes, dsem):
        self.eng = eng
        self.fn = fn
        self.reads = reads
        self.writes = writes
        self.dsem = dsem
        self.deps = ()
        self.signal = dsem is not None
        self.value = 0


class Prog:
    ENGS = ("pe", "act", "dve", "pool", "sp")

    def __init__(self):
        self.ops = []

    def add(self, eng, fn, reads=(), writes=(), dsem=None):
        self.ops.append(Op(eng, fn, tuple(reads), tuple(writes), dsem))

    def resolve(self):
        last_writer = {}
        readers = {}
        ops = self.ops
        for i, op in enumerate(ops):
            deps = set()
            for r in op.reads:
                w = last_writer.get(r)
                if w is not None:
                    deps.add(w)
            for r in op.writes:
                w = last_writer.get(r)
                if w is not None:
                    deps.add(w)
                for rd in readers.get(r, ()):
                    deps.add(rd)
            deps.discard(i)
            if op.eng == "pe":
                deps = {d for d in deps if ops[d].eng != "pe"}
            best = {}
            for d in deps:
                key = ops[d].dsem if ops[d].dsem is not None else ops[d].eng
                if d > best.get(key, -1):
                    best[key] = d
            op.deps = tuple(sorted(best.values()))
            for d in op.deps:
                ops[d].signal = True
            for r in op.reads:
                readers.setdefault(r, []).append(i)
            for r in op.writes:
                last_writer[r] = i
                readers[r] = []
        counters = {}
        for op in ops:
            if op.signal:
                key = op.dsem if op.dsem is not None else op.eng
                inc = 16 if op.dsem is not None else 1
                counters[key] = counters.get(key, 0) + inc
                op.value = counters[key]
        return counters

    def sem_keys(self):
        keys = []
        for op in self.ops:
            if op.signal:
                key = op.dsem if op.dsem is not None else op.eng
                if key not in keys:
                    keys.append(key)
        return keys

    def emit(self, engname, e, sems):
        ops = self.ops
        seen = {}
        for op in ops:
            if op.eng != engname:
                continue
            need = {}
            for d in op.deps:
                dop = ops[d]
                key = dop.dsem if dop.dsem is not None else dop.eng
                if dop.value > need.get(key, 0):
                    need[key] = dop.value
            for key, val in need.items():
                if seen.get(key, 0) < val:
                    e.wait_ge(sems[key], val)
                    seen[key] = val
            if op.fn is None:
                continue
            ins = op.fn(e)
            if op.signal:
                key = op.dsem if op.dsem is not None else op.eng
                ins.then_inc(sems[key], 16 if op.dsem is not None else 1)


def build_program(layers, final_norm):
    n_tiles = sum(TILES_A if l % 2 == 0 else TILES_B for l in layers)
    nc = bass.Bass("TRN2", target_bir_lowering=False)
    x_d = nc.dram_tensor("x", [128, NPASS, NTB, KD, TB], F32, kind="ExternalInput").ap()
    wt_d = nc.dram_tensor("wt", [n_tiles, 128, 4096], F32, kind="ExternalInput").ap()
    cst_d = nc.dram_tensor("cst", [128, NCST], F32, kind="ExternalInput").ap()
    out_d = nc.dram_tensor("out", [128, NPASS, NTB, KD, TB], F32, kind="ExternalOutput").ap()

    P = Prog()
    es = ExitStack()
    with es:
        es.enter_context(nc.allow_low_precision("bf16 matmul operands, fp32 accumulation"))

        def sb(name, shape, dt):
            return es.enter_context(nc.sbuf_tensor(name, shape, dt))

        x_sb = sb("x_sb", [128, KD, T], F32)
        hT = sb("hT", [128, KD, T], BF16)
        y_sb = sb("y_sb", [128, KI, T], BF16)
        v_sb = sb("v_sb", [128, NTC, DI], BF16)
        wslot = [sb(f"wslot{i}", [128, 4096], BF16) for i in range(NSLOT)]
        cst = sb("cst_sb", [128, NCST], F32)
        eps_t = sb("eps_t", [128, 1], F32)
        sq = [sb(f"sq{i}", [128, TB], F32) for i in range(3)]
        acc = [sb(f"acc{i}", [128, TB], F32) for i in range(2)]
        acc_hi = [sb(f"acchi{i}", [128, TB], BF16) for i in range(2)]
        acc_lo = [sb(f"acclo{i}", [128, TB], BF16) for i in range(2)]
        ones_bf = sb("ones_bf", [128, 128], BF16)
        sqb = [sb(f"sqb{i}", [128, TB], BF16) for i in range(4)]
        rstd = [sb(f"rstd{i}", [128, TB], F32) for i in range(2)]
        rt = rstd
        junk = [sb(f"junk{i}", [128, TB], BF16) for i in range(2)]
        ssv = sb("ssv", [128, NTC * 4], F32)
        ssvs = sb("ssvs", [128, NTC], F32)
        rtv = sb("rtv", [128, NTC], F32)
        rstdv = sb("rstdv", [128, NTC], F32)
        wsTs = [sb(f"wsTs{i}", [128, NTC, 128], BF16) for i in range(2)]
        tmp_sz = [sb(f"tsz{i}", [128, TB], F32) for i in range(2)]
        tmp_m = [sb(f"tm{i}", [128, TB], F32) for i in range(2)]
        tmp_c = tmp_m
        tmp_a = [sb(f"ta{i}", [128, TB], F32) for i in range(2)]
        xcb = [sb(f"xc{i}", [128, TB + 2], F32) for i in range(2)]
        halo = sb("halo", [128, 2, KI, 2], F32)
        banks = [es.enter_context(nc.psum_tensor(f"bank{i}", [128, 512], F32)) for i in range(8)]

        state = {"bank": 0, "tile": 0, "slot": 0, "sq": 0, "sqb": 0, "nload": 0}

        def next_bank():
            b = state["bank"]
            state["bank"] = (b + 1) % 8
            return b

        def tbs(tb):
            return slice(tb * TB, (tb + 1) * TB)

        def load_wtile(extra_reads=()):
            s = state["slot"]
            state["slot"] = (s + 1) % NSLOT
            idx = state["tile"]
            state["tile"] += 1
            extra = tuple(extra_reads)
            first = state["nload"] < NSLOT
            if idx == 0 and not first:
                extra = extra + (("x", 0, 0),)
            if first:
                extra = extra + ((("w", s - 1),) if state["nload"] > 0 else (("x", 0, 0),))
            state["nload"] += 1
            P.add("pool", lambda e, s=s, idx=idx: e.dma_start(out=wslot[s][:, :], in_=wt_d[idx]),
                  reads=extra, writes=(("w", s),), dsem=("w", s))
            if state["nload"] == 1:
                input_x(0, 1, extra_reads=(("w", s),))
            if state["nload"] == NSLOT:
                load_big_consts()
            return s

        yo = v_sb[:, :, :].bitcast(F32)

        def input_x(h, tb, extra_reads=()):
            P.add("sp", lambda e, tb=tb, h=h: e.dma_start(out=x_sb[:, :, tbs(tb)], in_=x_d[:, h, tb, :, :]),
                  reads=tuple(extra_reads), writes=tuple(("x", k, tb) for k in range(KD)), dsem=("ld", tb))

        def output_y(h, tb, kh):
            k0 = kh * 4
            P.add("sp", lambda e, tb=tb, h=h, k0=k0: e.dma_start(out=out_d[:, h, tb, k0:k0 + 4, :], in_=yo[:, k0:k0 + 4, tbs(tb)]),
                  reads=tuple(("v", k, 2 * tb + i) for k in range(k0, k0 + 4) for i in range(2)), dsem=("st", tb, kh))

        P.add("sp", lambda e: e.dma_start(out=cst[:, 0:C_WS], in_=cst_d[:, 0:C_WS]), writes=(("cst",),), dsem=("cst",))
        input_x(0, 0)
        P.add("dve", lambda e: e.memset(ones_bf[:, :], 1.0), writes=(("onesb",),))
        P.add("dve", lambda e: e.memset(eps_t[:, :], EPS), writes=(("eps",),))
        P.add("dve", lambda e: e.memset(halo[:, :, :, :], 0.0),
              writes=tuple(("halo", lb, j) for lb in range(2) for j in range(KI)))

        def load_big_consts():
            P.add("sp", lambda e: e.dma_start(out=cst[:, C_WS:NCST], in_=cst_d[:, C_WS:NCST]),
                  reads=(("w", NSLOT - 1),), writes=(("cstb",),), dsem=("cstb",))
            for la in range(2):
                if 2 * la in layers:
                    ws_ap = cst[:, C_WS + la * 1024:C_WS + (la + 1) * 1024].rearrange("p (g t) -> p g t", g=8)
                    mk_ap = cst[:, C_MK:C_MK + 128].unsqueeze(1).broadcast_to([128, 8, 128])
                    P.add("dve", lambda e, ws_ap=ws_ap, mk_ap=mk_ap: e.tensor_tensor(out=ws_ap, in0=ws_ap, in1=mk_ap, op=ALU.mult),
                          reads=(("cst",), ("cstb",)), writes=(("wsTm", la),))

        def norm_acc(tb, k):
            if k == 0:
                P.add("act", lambda e, tb=tb: e.activation(out=acc[tb][:, :], in_=x_sb[:, 0, tbs(tb)], func=AF.Square),
                      reads=(("x", 0, tb),), writes=(("acc", tb),))
                return
            i = state["sq"] % 3
            state["sq"] += 1
            P.add("act", lambda e, i=i, k=k, tb=tb: e.activation(out=sq[i][:, :], in_=x_sb[:, k, tbs(tb)], func=AF.Square),
                  reads=(("x", k, tb),), writes=(("sq", i),))
            P.add("dve", lambda e, i=i, tb=tb: e.tensor_tensor(out=acc[tb][:, :], in0=acc[tb][:, :], in1=sq[i][:, :], op=ALU.add),
                  reads=(("acc", tb), ("sq", i)), writes=(("acc", tb),))

        def norm_fin(tb, gcol, to_x, pe_acc=False):
            b = next_bank()
            if pe_acc:
                for k in range(KD):
                    i = state["sqb"] % 4
                    state["sqb"] += 1
                    P.add("act", lambda e, i=i, k=k, tb=tb: e.activation(out=sqb[i][:, :], in_=x_sb[:, k, tbs(tb)], func=AF.Square),
                          reads=(("x", k, tb),), writes=(("sqb", i),))
                    P.add("pe", lambda e, b=b, i=i, k=k: e.matmul(banks[b][:, :], lhsT=ones_bf[:, :], rhs=sqb[i][:, :],
                                                                  start=(k == 0), stop=(k == KD - 1)),
                          reads=(("sqb", i), ("onesb",)), writes=(("bank", b),))
            else:
                P.add("act", lambda e, tb=tb: e.activation(out=acc_hi[tb][:, :], in_=acc[tb][:, :], func=AF.Copy),
                      reads=(("acc", tb),), writes=(("acchi", tb),))
                P.add("dve", lambda e, tb=tb: e.tensor_tensor(out=acc_lo[tb][:, :], in0=acc[tb][:, :], in1=acc_hi[tb][:, :], op=ALU.subtract),
                      reads=(("acc", tb), ("acchi", tb)), writes=(("acclo", tb),))
                P.add("pe", lambda e, b=b, tb=tb: e.matmul(banks[b][:, :], lhsT=ones_bf[:, :], rhs=acc_hi[tb][:, :], start=True, stop=False),
                      reads=(("acchi", tb), ("onesb",)), writes=(("bank", b),))
                P.add("pe", lambda e, b=b, tb=tb: e.matmul(banks[b][:, :], lhsT=ones_bf[:, :], rhs=acc_lo[tb][:, :], start=False, stop=True),
                      reads=(("acclo", tb), ("onesb",)), writes=(("bank", b),))
            P.add("act", lambda e, b=b, tb=tb: e.activation(out=rt[tb][:, :], in_=banks[b][:, :], func=AF.Sqrt,
                                                            scale=1.0 / D, bias=eps_t[:, 0:1]),
                  reads=(("bank", b), ("eps",)), writes=(("rstd", tb),))
            P.add("dve", lambda e, tb=tb: e.reciprocal(out=rstd[tb][:, :], in_=rt[tb][:, :]),
                  reads=(("rstd", tb),), writes=(("rstd", tb),))
            for k in range(KD):
                if to_x:
                    out_fn = lambda k=k, tb=tb: yo[:, k, tbs(tb)]
                    wr = (("v", k, 2 * tb), ("v", k, 2 * tb + 1))
                else:
                    out_fn = lambda k=k, tb=tb: hT[:, k, tbs(tb)]
                    wr = (("hT", k, tb),)
                P.add("dve", lambda e, k=k, tb=tb, out_fn=out_fn: e.scalar_tensor_tensor(
                    out=out_fn(), in0=x_sb[:, k, tbs(tb)], scalar=cst[:, gcol + k:gcol + k + 1],
                    in1=rstd[tb][:, :], op0=ALU.mult, op1=ALU.mult),
                    reads=(("x", k, tb), ("rstd", tb), ("cst",)), writes=wr)
                if to_x and k % 4 == 3:
                    output_y(state["h"], tb, k // 4)

        def out_proj(hook):
            slots = []
            for tb in range(NTB):
                for dp in range(4):
                    if tb == 0:
                        slots.append(load_wtile())
                    s = slots[dp]
                    W = wslot[s][:, :].rearrange("p (d k c) -> p d k c", d=2, k=KI)
                    for dd in range(2):
                        dmc = 2 * dp + dd
                        b = next_bank()
                        for k in range(KI):
                            P.add("pe", lambda e, b=b, W=W, dd=dd, k=k, tb=tb: e.matmul(
                                banks[b][:, :], lhsT=W[:, dd, k, :], rhs=y_sb[:, k, tbs(tb)],
                                start=(k == 0), stop=(k == KI - 1)),
                                reads=(("w", s), ("y", k, tb)), writes=(("bank", b),))
                        P.add("dve", lambda e, b=b, dmc=dmc, tb=tb: e.tensor_tensor(
                            out=x_sb[:, dmc, tbs(tb)], in0=banks[b][:, :], in1=x_sb[:, dmc, tbs(tb)], op=ALU.add),
                            reads=(("bank", b), ("x", dmc, tb)), writes=(("x", dmc, tb),))
                        norm_acc(tb, dmc)
                    if tb == 1 and dp == 0 and hook is not None:
                        hook()

        def layer_a(la, pending):
            slots = []
            for half in range(2):
                for nb in range(4):
                    if half == 0:
                        slots.append(load_wtile())
                    s = slots[nb]
                    W = wslot[s][:, :].rearrange("p (k c) -> p k c", k=KD)
                    for tc in range(half * 4, half * 4 + 4):
                        b = next_bank()
                        for k in range(KD):
                            P.add("pe", lambda e, b=b, W=W, k=k, tc=tc: e.matmul(
                                banks[b][:, :], lhsT=hT[:, k, tc * 128:(tc + 1) * 128], rhs=W[:, k, :],
                                start=(k == 0), stop=(k == KD - 1)),
                                reads=(("w", s), ("hT", k, tc // 4)), writes=(("bank", b),))
                        col = tc * 4 + nb
                        ji = col % 2
                        P.add("act", lambda e, b=b, col=col, ji=ji: e.activation(out=junk[ji][:, :], in_=banks[b][:, :], func=AF.Square,
                                                                                 accum_out=ssv[:, col:col + 1]),
                              reads=(("bank", b),), writes=(("ssv", col), ("junk", ji)))
                        P.add("act", lambda e, b=b, tc=tc, nb=nb: e.activation(out=v_sb[:, tc, nb * 512:(nb + 1) * 512], in_=banks[b][:, :], func=AF.Copy),
                              reads=(("bank", b),), writes=(("v", tc, nb),))
                    if half == 0 and nb in pending:
                        pending[nb]()
            P.add("dve", lambda e: e.tensor_reduce(out=ssvs[:, :], in_=ssv[:, :].rearrange("p (a c) -> p a c", c=4),
                                                   axis=AX.X, op=ALU.add),
                  reads=tuple(("ssv", c) for c in range(NTC * 4)), writes=(("ssvs",),))
            P.add("act", lambda e: e.activation(out=rtv[:, :], in_=ssvs[:, :], func=AF.Sqrt, scale=1.0 / DI, bias=eps_t[:, 0:1]),
                  reads=(("ssvs",), ("eps",)), writes=(("rtv",),))
            P.add("dve", lambda e: e.reciprocal(out=rstdv[:, :], in_=rtv[:, :]),
                  reads=(("rtv",),), writes=(("rstdv",),))
            unit = 0
            for jp in range(8):
                s = load_wtile()
                W = wslot[s][:, :].rearrange("p (a m k c) -> p a m k c", a=2, m=2, k=KD)
                g = jp
                wb = wsTs[g % 2]
                ws_g = cst[:, C_WS + la * 1024 + g * 128:C_WS + la * 1024 + (g + 1) * 128]
                P.add("dve", lambda e, wb=wb, ws_g=ws_g: e.tensor_tensor(
                    out=wb[:, :, :], in0=ws_g.unsqueeze(1).broadcast_to([128, NTC, 128]),
                    in1=rstdv[:, :].unsqueeze(2).broadcast_to([128, NTC, 128]), op=ALU.mult),
                    reads=(("wsTm", la), ("rstdv",)), writes=(("wsTs", g % 2),))
                bb_g = cst[:, C_BB + la * 1024 + g * 128:C_BB + la * 1024 + (g + 1) * 128]
                for jj in range(2):
                    j = 2 * jp + jj
                    gcol = C_GV + la * 16 + j
                    for tb in range(NTB):
                        bu, bz, bm = next_bank(), next_bank(), next_bank()
                        ts = unit % 2
                        unit += 1
                        for (bk, m) in ((bz, 1), (bu, 0)):
                            for k in range(KD):
                                P.add("pe", lambda e, bk=bk, W=W, jj=jj, m=m, k=k, tb=tb: e.matmul(
                                    banks[bk][:, :], lhsT=W[:, jj, m, k, :], rhs=hT[:, k, tbs(tb)],
                                    start=(k == 0), stop=(k == KD - 1)),
                                    reads=(("w", s), ("hT", k, tb)), writes=(("bank", bk),))
                        for q in range(4):
                            tc = tb * 4 + q
                            P.add("pe", lambda e, bm=bm, q=q, tc=tc, j=j, wb=wb: e.matmul(
                                banks[bm][:, q * 128:(q + 1) * 128], lhsT=v_sb[:, tc, j * 128:(j + 1) * 128],
                                rhs=wb[:, tc, :], start=True, stop=True),
                                reads=(("v", tc, j // 4), ("wsTs", g % 2)), writes=(("bank", bm),))
                        P.add("act", lambda e, bz=bz, ts=ts: e.activation(out=tmp_sz[ts][:, :], in_=banks[bz][:, :], func=AF.Silu),
                              reads=(("bank", bz),), writes=(("tsz", ts),))
                        P.add("dve", lambda e, bu=bu, ts=ts: e.tensor_tensor(
                            out=tmp_sz[ts][:, :], in0=banks[bu][:, :], in1=tmp_sz[ts][:, :], op=ALU.mult),
                            reads=(("bank", bu), ("tsz", ts)), writes=(("tsz", ts),))
                        P.add("dve", lambda e, bm=bm, ts=ts, gcol=gcol, bb_g=bb_g: e.scalar_tensor_tensor(
                            out=tmp_m[ts][:, :].rearrange("p (a c) -> p a c", a=4),
                            in0=banks[bm][:, :].rearrange("p (a c) -> p a c", a=4),
                            scalar=cst[:, gcol:gcol + 1],
                            in1=bb_g.unsqueeze(1).broadcast_to([128, 4, 128]),
                            op0=ALU.mult, op1=ALU.add),
                            reads=(("bank", bm), ("cst",), ("cstb",)), writes=(("tm", ts),))
                        P.add("dve", lambda e, ts=ts, j=j, tb=tb: e.tensor_tensor(
                            out=y_sb[:, j, tbs(tb)], in0=tmp_sz[ts][:, :], in1=tmp_m[ts][:, :], op=ALU.mult),
                            reads=(("tsz", ts), ("tm", ts)), writes=(("y", j, tb),))

        def layer_b(lb, pending):
            ucount = [0]

            def b_unit(j, tb, s):
                W = wslot[s][:, :].rearrange("p (m k c) -> p m k c", m=4, k=KD)
                wc = [C_WC + lb * 48 + tap * 16 + j for tap in range(3)]
                bC, bX, bZ, bB = next_bank(), next_bank(), next_bank(), next_bank()
                ts = ucount[0] % 2
                ucount[0] += 1
                for (bk, m) in ((bC, 1), (bX, 2), (bZ, 3), (bB, 0)):
                    for k in range(KD):
                        P.add("pe", lambda e, bk=bk, W=W, m=m, k=k, tb=tb: e.matmul(
                            banks[bk][:, :], lhsT=W[:, m, k, :], rhs=hT[:, k, tbs(tb)],
                            start=(k == 0), stop=(k == KD - 1)),
                            reads=(("w", s), ("hT", k, tb)), writes=(("bank", bk),))
                P.add("act", lambda e, j=j, ts=ts: e.activation(out=xcb[ts][:, 0:2], in_=halo[:, lb, j, :], func=AF.Copy),
                      reads=(("halo", lb, j),), writes=(("xch", ts),))
                P.add("act", lambda e, bC=bC, ts=ts: e.activation(out=tmp_c[ts][:, :], in_=banks[bC][:, :], func=AF.Copy),
                      reads=(("bank", bC),), writes=(("tm", ts),))
                P.add("dve", lambda e, bX=bX, ts=ts: e.tensor_tensor(
                    out=xcb[ts][:, 2:TB + 2], in0=banks[bX][:, :], in1=tmp_c[ts][:, :], op=ALU.mult),
                    reads=(("bank", bX), ("tm", ts)), writes=(("xc", ts),))
                P.add("act", lambda e, j=j, ts=ts: e.activation(out=halo[:, lb, j, :], in_=xcb[ts][:, TB:TB + 2], func=AF.Copy),
                      reads=(("xc", ts),), writes=(("halo", lb, j),))
                P.add("act", lambda e, ts=ts, c=wc[2]: e.activation(
                    out=tmp_a[ts][:, :], in_=xcb[ts][:, 2:TB + 2], func=AF.Copy, scale=cst[:, c:c + 1]),
                    reads=(("xc", ts), ("cst",)), writes=(("ta", ts),))
                P.add("act", lambda e, bZ=bZ, ts=ts: e.activation(out=tmp_sz[ts][:, :], in_=banks[bZ][:, :], func=AF.Silu),
                      reads=(("bank", bZ),), writes=(("tsz", ts),))
                P.add("dve", lambda e, ts=ts, c=wc[1]: e.scalar_tensor_tensor(
                    out=tmp_a[ts][:, :], in0=xcb[ts][:, 1:TB + 1], scalar=cst[:, c:c + 1], in1=tmp_a[ts][:, :],
                    op0=ALU.mult, op1=ALU.add),
                    reads=(("xc", ts), ("xch", ts), ("ta", ts), ("cst",)), writes=(("ta", ts),))
                P.add("dve", lambda e, ts=ts, c=wc[0]: e.scalar_tensor_tensor(
                    out=tmp_c[ts][:, :], in0=xcb[ts][:, 0:TB], scalar=cst[:, c:c + 1], in1=tmp_a[ts][:, :],
                    op0=ALU.mult, op1=ALU.add),
                    reads=(("xc", ts), ("xch", ts), ("ta", ts), ("cst",)), writes=(("tm", ts),))
                P.add("dve", lambda e, bB=bB, ts=ts: e.tensor_tensor(
                    out=tmp_sz[ts][:, :], in0=banks[bB][:, :], in1=tmp_sz[ts][:, :], op=ALU.mult),
                    reads=(("bank", bB), ("tsz", ts)), writes=(("tsz", ts),))
                P.add("dve", lambda e, ts=ts, j=j, tb=tb: e.tensor_tensor(
                    out=y_sb[:, j, tbs(tb)], in0=tmp_sz[ts][:, :], in1=tmp_c[ts][:, :], op=ALU.mult),
                    reads=(("tsz", ts), ("tm", ts)), writes=(("y", j, tb),))

            slots = []
            for j in range(4):
                slots.append(load_wtile())
                b_unit(j, 0, slots[j])
                if j in pending:
                    pending[j]()
            for j in range(4):
                b_unit(j, 1, slots[j])
            for j in range(4, KI):
                s = load_wtile()
                for tb in range(NTB):
                    b_unit(j, tb, s)

        nl = len(layers)
        assert layers[-1] % 2 == 1, 'final-norm staging reuses v_sb: last layer must be a short-conv layer'
        def acc_and_fin_tb1(l):
            for k in range(KD):
                norm_acc(1, k)
            norm_fin(1, C_GN + l * 8, to_x=False)

        for h in range(NPASS):
            state["tile"] = 0
            state["h"] = h
            l0 = layers[0]
            if h > 0:
                state["h"] = h - 1
                norm_fin(1, C_GF, to_x=True)
                state["h"] = h
                input_x(h, 1)
            norm_fin(0, C_GN + l0 * 8, to_x=False, pe_acc=True)
            for li, l in enumerate(layers):
                if li == 0 and h == 0:
                    pending = {1: (lambda: [norm_acc(1, k) for k in range(KD)]),
                               2: (lambda l=l: norm_fin(1, C_GN + l * 8, to_x=False))}
                elif li == 0:
                    pending = {1: (lambda l=l: norm_fin(1, C_GN + l * 8, to_x=False, pe_acc=True))}
                else:
                    pending = {0: (lambda l=l: norm_fin(1, C_GN + l * 8, to_x=False))}
                if l % 2 == 0:
                    layer_a(l // 2, pending)
                else:
                    layer_b(l // 2, pending)
                if li + 1 < nl:
                    hook = (lambda ln=layers[li + 1]: norm_fin(0, C_GN + ln * 8, to_x=False))
                else:
                    def hook(h=h):
                        norm_fin(0, C_GF, to_x=True)
                        if h + 1 < NPASS:
                            input_x(h + 1, 0)
                out_proj(hook)
        norm_fin(1, C_GF, to_x=True)
        P.add("sp", None, writes=tuple(("v", k, i) for k in range(NTC) for i in range(4)))

        P.resolve()
        keys = P.sem_keys()
        sems = {}
        for i, key in enumerate(keys):
            sems[key] = es.enter_context(nc.semaphore(f"s{i}"))
        block = es.enter_context(nc.Block())

        @block.tensor
        def _(e):
            P.emit("pe", e, sems)

        @block.scalar
        def _(e):
            P.emit("act", e, sems)

        @block.vector
        def _(e):
            P.emit("dve", e, sems)

        @block.gpsimd
        def _(e):
            P.emit("pool", e, sems)

        @block.sync
        def _(e):
            P.emit("sp", e, sems)
    return nc


def _pkc(w):
    k = w.shape[0] // 128
    return np.ascontiguousarray(w.reshape(k, 128, w.shape[1]).transpose(1, 0, 2))


def _o_tiles(w_out):
    wr = _pkc(w_out)
    t = wr.reshape(128, KI, 4, 2, 128)
    t = t.transpose(2, 0, 3, 1, 4)
    return np.ascontiguousarray(t).reshape(4, 128, 4096)


def _a_tiles(w_in, w_out):
    wr = _pkc(w_in)
    v = wr[:, :, DI:2 * DI].reshape(128, KD, 4, 512).transpose(2, 0, 1, 3)
    a1 = np.ascontiguousarray(v).reshape(4, 128, 4096)
    u = wr[:, :, 0:DI].reshape(128, KD, 8, 2, 128)
    z = wr[:, :, 2 * DI:3 * DI].reshape(128, KD, 8, 2, 128)
    uz = np.stack([u, z], axis=0)
    a2 = np.ascontiguousarray(uz.transpose(3, 1, 4, 0, 2, 5)).reshape(8, 128, 4096)
    return np.concatenate([a1, a2, _o_tiles(w_out)], axis=0)


def _b_tiles(w_in, w_out):
    wr = _pkc(w_in)
    t = wr.reshape(128, KD, 4, KI, 128)
    b1 = np.ascontiguousarray(t.transpose(3, 0, 2, 1, 4)).reshape(KI, 128, 4096)
    return np.concatenate([b1, _o_tiles(w_out)], axis=0)


def _consts(norm_g, final_g, a_v_norm_g, a_w_s, a_b_s, b_w_conv):
    c = np.zeros((128, NCST), np.float32)
    c[:, C_GN:C_GN + 32] = norm_g.reshape(4, KD, 128).transpose(2, 0, 1).reshape(128, 32)
    c[:, C_GF:C_GF + 8] = final_g.reshape(KD, 128).T
    c[:, C_GV:C_GV + 32] = a_v_norm_g.reshape(2, KI, 128).transpose(2, 0, 1).reshape(128, 32)
    c[:, C_WC:C_WC + 96] = b_w_conv.reshape(2, 3, KI, 128).transpose(3, 0, 1, 2).reshape(128, 96)
    c[:, C_ID:C_ID + 128] = np.eye(128, dtype=np.float32)
    c[:, C_MK:C_MK + 128] = np.triu(np.ones((128, 128), np.float32))
    c[:, C_WS:C_WS + 2048] = a_w_s.transpose(3, 0, 1, 2).reshape(128, 2048)
    c[:, C_BB:C_BB + 2048] = np.broadcast_to(a_b_s.reshape(1, 2048), (128, 2048))
    return c


LAUNCH_GROUPS = [[0, 1, 2, 3]]
_prog_cache = {}


def kernel(x, norm_g, final_g, a_w_in, a_v_norm_g, a_w_s, a_b_s, a_w_out, b_w_in, b_w_conv, b_w_out):
    f = lambda a: np.asarray(a, dtype=np.float32)
    x = f(x)
    cst = _consts(f(norm_g), f(final_g), f(a_v_norm_g), f(a_w_s), f(a_b_s), f(b_w_conv))
    tiles = {}
    for l in range(4):
        if l % 2 == 0:
            tiles[l] = _a_tiles(f(a_w_in)[l // 2], f(a_w_out)[l // 2])
        else:
            tiles[l] = _b_tiles(f(b_w_in)[l // 2], f(b_w_out)[l // 2])
    cur = [np.ascontiguousarray(x[c].reshape(NPASS, NTB, TB, KD, 128).transpose(4, 0, 1, 3, 2)) for c in range(N_CORES)]
    for gi, group in enumerate(LAUNCH_GROUPS):
        last = gi == len(LAUNCH_GROUPS) - 1
        key = (tuple(group), last)
        if key not in _prog_cache:
            _prog_cache[key] = build_program(group, final_norm=last)
        nc = _prog_cache[key]
        wt = np.concatenate([tiles[l] for l in group], axis=0)
        in_maps = [{"x": cur[c], "wt": wt, "cst": cst} for c in range(N_CORES)]
        res = run_bass_kernel_spmd(nc, in_maps, core_ids=list(range(N_CORES)))
        cur = [np.asarray(res.results[c]["out"], dtype=np.float32) for c in range(N_CORES)]
    o = np.stack(cur, axis=0)
    return np.ascontiguousarray(o.transpose(0, 2, 3, 5, 4, 1)).reshape(N_CORES, SEQ, D)
```

```python
import numpy as np
from contextlib import ExitStack

import concourse.bass as bass
import concourse.mybir as mybir
from concourse.bass_utils import run_bass_kernel_spmd

F32 = mybir.dt.float32
F32R = mybir.dt.float32r
BF16 = mybir.dt.bfloat16
AF = mybir.ActivationFunctionType
ALU = mybir.AluOpType
AX = mybir.AxisListType

D = 1024
DI = 2048
SEQ = 2048
T = 1024
NPASS = SEQ // T
TB = 512
NTB = T // TB
NTC = T // 128
KD = D // 128
KI = DI // 128
EPS = 1e-6
N_CORES = 8

C_GN = 0
C_GF = C_GN + 32
C_GV = C_GF + 8
C_WC = C_GV + 32
C_ID = C_WC + 96
C_MK = C_ID + 128
C_WS = C_MK + 128
C_BB = C_WS + 2048
NCST = C_BB + 2048

NSLOT = 4
TILES_A = 16
TILES_B = 20


class Op:
    __slots__ = ("eng", "fn", "reads", "writes", "dsem", "deps", "signal", "value")

    def __init__(self, eng, fn, reads, writes, dsem):
        self.eng = eng
        self.fn = fn
        self.reads = reads
        self.writes = writes
        self.dsem = dsem
        self.deps = ()
        self.signal = dsem is not None
        self.value = 0


class Prog:
    ENGS = ("pe", "act", "dve", "pool", "sp")

    def __init__(self):
        self.ops = []

    def add(self, eng, fn, reads=(), writes=(), dsem=None):
        self.ops.append(Op(eng, fn, tuple(reads), tuple(writes), dsem))

    def resolve(self):
        last_writer = {}
        readers = {}
        ops = self.ops
        for i, op in enumerate(ops):
            deps = set()
            for r in op.reads:
                w = last_writer.get(r)
                if w is not None:
                    deps.add(w)
            for r in op.writes:
                w = last_writer.get(r)
                if w is not None:
                    deps.add(w)
                for rd in readers.get(r, ()):
                    deps.add(rd)
            deps.discard(i)
            if op.eng == "pe":
                deps = {d for d in deps if ops[d].eng != "pe"}
            best = {}
            for d in deps:
                key = ops[d].dsem if ops[d].dsem is not None else ops[d].eng
                if d > best.get(key, -1):
                    best[key] = d
            op.deps = tuple(sorted(best.values()))
            for d in op.deps:
                ops[d].signal = True
            for r in op.reads:
                readers.setdefault(r, []).append(i)
            for r in op.writes:
                last_writer[r] = i
                readers[r] = []
        counters = {}
        for op in ops:
            if op.signal:
                key = op.dsem if op.dsem is not None else op.eng
                inc = 16 if op.dsem is not None else 1
                counters[key] = counters.get(key, 0) + inc
                op.value = counters[key]
        return counters

    def sem_keys(self):
        keys = []
        for op in self.ops:
            if op.signal:
                key = op.dsem if op.dsem is not None else op.eng
                if key not in keys:
                    keys.append(key)
        return keys

    def emit(self, engname, e, sems):
        ops = self.ops
        seen = {}
        for op in ops:
            if op.eng != engname:
                continue
            need = {}
            for d in op.deps:
                dop = ops[d]
                key = dop.dsem if dop.dsem is not None else dop.eng
                if dop.value > need.get(key, 0):
                    need[key] = dop.value
            for key, val in need.items():
                if seen.get(key, 0) < val:
                    e.wait_ge(sems[key], val)
                    seen[key] = val
            if op.fn is None:
                continue
            ins = op.fn(e)
            if op.signal:
                key = op.dsem if op.dsem is not None else op.eng
                ins.then_inc(sems[key], 16 if op.dsem is not None else 1)


def build_program(layers, final_norm):
    n_tiles = sum(TILES_A if l % 2 == 0 else TILES_B for l in layers)
    nc = bass.Bass("TRN2", target_bir_lowering=False)
    x_d = nc.dram_tensor("x", [128, NPASS, NTB, KD, TB], F32, kind="ExternalInput").ap()
    wt_d = nc.dram_tensor("wt", [n_tiles, 128, 4096], F32, kind="ExternalInput").ap()
    cst_d = nc.dram_tensor("cst", [128, NCST], F32, kind="ExternalInput").ap()
    out_d = nc.dram_tensor("out", [128, NPASS, NTB, KD, TB], F32, kind="ExternalOutput").ap()

    P = Prog()
    es = ExitStack()
    with es:
        es.enter_context(nc.allow_low_precision("bf16 matmul operands, fp32 accumulation"))

        def sb(name, shape, dt):
            return es.enter_context(nc.sbuf_tensor(name, shape, dt))

        x_sb = sb("x_sb", [128, KD, T], F32)
        hT = sb("hT", [128, KD, T], BF16)
        y_sb = sb("y_sb", [128, KI, T], BF16)
        v_sb = sb("v_sb", [128, NTC, DI], BF16)
        wslot = [sb(f"wslot{i}", [128, 4096], BF16) for i in range(NSLOT)]
        cst = sb("cst_sb", [128, NCST], F32)
        eps_t = sb("eps_t", [128, 1], F32)
        sq = [sb(f"sq{i}", [128, TB], F32) for i in range(3)]
        acc = [sb(f"acc{i}", [128, TB], F32) for i in range(2)]
        ones_f = sb("ones_f", [128, 128], F32)
        ones_bf = sb("ones_bf", [128, 128], BF16)
        sqb = [sb(f"sqb{i}", [128, TB], BF16) for i in range(4)]
        rstd = [sb(f"rstd{i}", [128, TB], F32) for i in range(2)]
        rt = rstd
        junk = [sb(f"junk{i}", [128, TB], BF16) for i in range(2)]
        ssv = sb("ssv", [128, NTC * 4], F32)
        ssvs = sb("ssvs", [128, NTC], F32)
        rtv = sb("rtv", [128, NTC], F32)
        rstdv = sb("rstdv", [128, NTC], F32)
        wsTs = [sb(f"wsTs{i}", [128, NTC, 128], BF16) for i in range(2)]
        tmp_sz = [sb(f"tsz{i}", [128, TB], F32) for i in range(2)]
        tmp_m = [sb(f"tm{i}", [128, TB], F32) for i in range(2)]
        tmp_c = tmp_m
        tmp_a = [sb(f"ta{i}", [128, TB], F32) for i in range(2)]
        xcb = [sb(f"xc{i}", [128, TB + 2], F32) for i in range(2)]
        halo = sb("halo", [128, 2, KI, 2], F32)
        banks = [es.enter_context(nc.psum_tensor(f"bank{i}", [128, 512], F32)) for i in range(8)]

        state = {"bank": 0, "tile": 0, "slot": 0, "sq": 0, "sqb": 0, "nload": 0}

        def next_bank():
            b = state["bank"]
            state["bank"] = (b + 1) % 8
            return b

        def tbs(tb):
            return slice(tb * TB, (tb + 1) * TB)

        def load_wtile(extra_reads=()):
            s = state["slot"]
            state["slot"] = (s + 1) % NSLOT
            idx = state["tile"]
            state["tile"] += 1
            extra = tuple(extra_reads)
            first = state["nload"] < NSLOT
            if idx == 0 and not first:
                extra = extra + (("x", 0, 0),)
            if first:
                extra = extra + ((("w", s - 1),) if state["nload"] > 0 else (("x", 0, 0),))
            state["nload"] += 1
            P.add("pool", lambda e, s=s, idx=idx: e.dma_start(out=wslot[s][:, :], in_=wt_d[idx]),
                  reads=extra, writes=(("w", s),), dsem=("w", s))
            if state["nload"] == 1:
                input_x(0, 1, extra_reads=(("w", s),))
            if state["nload"] == NSLOT:
                load_big_consts()
            return s

        yo = v_sb[:, :, :].bitcast(F32)

        def input_x(h, tb, extra_reads=()):
            P.add("sp", lambda e, tb=tb, h=h: e.dma_start(out=x_sb[:, :, tbs(tb)], in_=x_d[:, h, tb, :, :]),
                  reads=tuple(extra_reads), writes=tuple(("x", k, tb) for k in range(KD)), dsem=("ld", tb))

        def output_y(h, tb, kh):
            k0 = kh * 4
            P.add("sp", lambda e, tb=tb, h=h, k0=k0: e.dma_start(out=out_d[:, h, tb, k0:k0 + 4, :], in_=yo[:, k0:k0 + 4, tbs(tb)]),
                  reads=tuple(("v", k, 2 * tb + i) for k in range(k0, k0 + 4) for i in range(2)), dsem=("st", tb, kh))

        P.add("sp", lambda e: e.dma_start(out=cst[:, 0:C_WS], in_=cst_d[:, 0:C_WS]), writes=(("cst",),), dsem=("cst",))
        input_x(0, 0)
        P.add("dve", lambda e: e.memset(acc[0][:, 0:128], 1.0), writes=(("acc", 0),))
        P.add("dve", lambda e: e.tensor_copy(out=ones_f[:, :].bitcast(F32R), in_=acc[0][:, 0:128]), reads=(("acc", 0),), writes=(("ones",),))
        P.add("dve", lambda e: e.memset(ones_bf[:, :], 1.0), writes=(("onesb",),))
        P.add("dve", lambda e: e.memset(eps_t[:, :], EPS), writes=(("eps",),))
        P.add("dve", lambda e: e.memset(halo[:, :, :, :], 0.0),
              writes=tuple(("halo", lb, j) for lb in range(2) for j in range(KI)))

        def load_big_consts():
            P.add("sp", lambda e: e.dma_start(out=cst[:, C_WS:NCST], in_=cst_d[:, C_WS:NCST]),
                  reads=(("w", NSLOT - 1),), writes=(("cstb",),), dsem=("cstb",))
            for la in range(2):
                if 2 * la in layers:
                    ws_ap = cst[:, C_WS + la * 1024:C_WS + (la + 1) * 1024].rearrange("p (g t) -> p g t", g=8)
                    mk_ap = cst[:, C_MK:C_MK + 128].unsqueeze(1).broadcast_to([128, 8, 128])
                    P.add("dve", lambda e, ws_ap=ws_ap, mk_ap=mk_ap: e.tensor_tensor(out=ws_ap, in0=ws_ap, in1=mk_ap, op=ALU.mult),
                          reads=(("cst",), ("cstb",)), writes=(("wsTm", la),))

        def norm_acc(tb, k):
            if k == 0:
                P.add("act", lambda e, tb=tb: e.activation(out=acc[tb][:, :].bitcast(F32R), in_=x_sb[:, 0, tbs(tb)], func=AF.Square),
                      reads=(("x", 0, tb),), writes=(("acc", tb),))
                return
            i = state["sq"] % 3
            state["sq"] += 1
            P.add("act", lambda e, i=i, k=k, tb=tb: e.activation(out=sq[i][:, :], in_=x_sb[:, k, tbs(tb)], func=AF.Square),
                  reads=(("x", k, tb),), writes=(("sq", i),))
            oap = (lambda tb=tb: acc[tb][:, :].bitcast(F32R))
            P.add("dve", lambda e, i=i, tb=tb, oap=oap: e.tensor_tensor(out=oap(), in0=acc[tb][:, :], in1=sq[i][:, :], op=ALU.add),
                  reads=(("acc", tb), ("sq", i)), writes=(("acc", tb),))

        def norm_fin(tb, gcol, to_x, pe_acc=False):
            b = next_bank()
            if pe_acc:
                for k in range(KD):
                    i = state["sqb"] % 4
                    state["sqb"] += 1
                    P.add("act", lambda e, i=i, k=k, tb=tb: e.activation(out=sqb[i][:, :], in_=x_sb[:, k, tbs(tb)], func=AF.Square),
                          reads=(("x", k, tb),), writes=(("sqb", i),))
                    P.add("pe", lambda e, b=b, i=i, k=k: e.matmul(banks[b][:, :], lhsT=ones_bf[:, :], rhs=sqb[i][:, :],
                                                                  start=(k == 0), stop=(k == KD - 1)),
                          reads=(("sqb", i), ("onesb",)), writes=(("bank", b),))
            else:
                P.add("pe", lambda e, b=b, tb=tb: e.matmul(banks[b][:, :], lhsT=ones_f[:, :].bitcast(F32R), rhs=acc[tb][:, :].bitcast(F32R), start=True, stop=True),
                      reads=(("acc", tb), ("ones",)), writes=(("bank", b),))
            P.add("act", lambda e, b=b, tb=tb: e.activation(out=rt[tb][:, :], in_=banks[b][:, :], func=AF.Sqrt,
                                                            scale=1.0 / D, bias=eps_t[:, 0:1]),
                  reads=(("bank", b), ("eps",)), writes=(("rstd", tb),))
            P.add("dve", lambda e, tb=tb: e.reciprocal(out=rstd[tb][:, :], in_=rt[tb][:, :]),
                  reads=(("rstd", tb),), writes=(("rstd", tb),))
            for k in range(KD):
                if to_x:
                    out_fn = lambda k=k, tb=tb: yo[:, k, tbs(tb)]
                    wr = (("v", k, 2 * tb), ("v", k, 2 * tb + 1))
                else:
                    out_fn = lambda k=k, tb=tb: hT[:, k, tbs(tb)]
                    wr = (("hT", k, tb),)
                P.add("dve", lambda e, k=k, tb=tb, out_fn=out_fn: e.scalar_tensor_tensor(
                    out=out_fn(), in0=x_sb[:, k, tbs(tb)], scalar=cst[:, gcol + k:gcol + k + 1],
                    in1=rstd[tb][:, :], op0=ALU.mult, op1=ALU.mult),
                    reads=(("x", k, tb), ("rstd", tb), ("cst",)), writes=wr)
                if to_x and k % 4 == 3:
                    output_y(state["h"], tb, k // 4)

        def out_proj(hook):
            slots = []
            for tb in range(NTB):
                for dp in range(4):
                    if tb == 0:
                        slots.append(load_wtile())
                    s = slots[dp]
                    W = wslot[s][:, :].rearrange("p (d k c) -> p d k c", d=2, k=KI)
                    for dd in range(2):
                        dmc = 2 * dp + dd
                        b = next_bank()
                        for k in range(KI):
                            P.add("pe", lambda e, b=b, W=W, dd=dd, k=k, tb=tb: e.matmul(
                                banks[b][:, :], lhsT=W[:, dd, k, :], rhs=y_sb[:, k, tbs(tb)],
                                start=(k == 0), stop=(k == KI - 1)),
                                reads=(("w", s), ("y", k, tb)), writes=(("bank", b),))
                        P.add("dve", lambda e, b=b, dmc=dmc, tb=tb: e.tensor_tensor(
                            out=x_sb[:, dmc, tbs(tb)], in0=banks[b][:, :], in1=x_sb[:, dmc, tbs(tb)], op=ALU.add),
                            reads=(("bank", b), ("x", dmc, tb)), writes=(("x", dmc, tb),))
                        norm_acc(tb, dmc)
                    if tb == 1 and dp == 0 and hook is not None:
                        hook()

        def layer_a(la, pending):
            slots = []
            for half in range(2):
                for nb in range(4):
                    if half == 0:
                        slots.append(load_wtile())
                    s = slots[nb]
                    W = wslot[s][:, :].rearrange("p (k c) -> p k c", k=KD)
                    for tc in range(half * 4, half * 4 + 4):
                        b = next_bank()
                        for k in range(KD):
                            P.add("pe", lambda e, b=b, W=W, k=k, tc=tc: e.matmul(
                                banks[b][:, :], lhsT=hT[:, k, tc * 128:(tc + 1) * 128], rhs=W[:, k, :],
                                start=(k == 0), stop=(k == KD - 1)),
                                reads=(("w", s), ("hT", k, tc // 4)), writes=(("bank", b),))
                        col = tc * 4 + nb
                        ji = col % 2
                        P.add("act", lambda e, b=b, col=col, ji=ji: e.activation(out=junk[ji][:, :], in_=banks[b][:, :], func=AF.Square,
                                                                                 accum_out=ssv[:, col:col + 1]),
                              reads=(("bank", b),), writes=(("ssv", col), ("junk", ji)))
                        P.add("act", lambda e, b=b, tc=tc, nb=nb: e.activation(out=v_sb[:, tc, nb * 512:(nb + 1) * 512], in_=banks[b][:, :], func=AF.Copy),
                              reads=(("bank", b),), writes=(("v", tc, nb),))
                    if half == 0 and nb in pending:
                        pending[nb]()
            P.add("dve", lambda e: e.tensor_reduce(out=ssvs[:, :], in_=ssv[:, :].rearrange("p (a c) -> p a c", c=4),
                                                   axis=AX.X, op=ALU.add),
                  reads=tuple(("ssv", c) for c in range(NTC * 4)), writes=(("ssvs",),))
            P.add("act", lambda e: e.activation(out=rtv[:, :], in_=ssvs[:, :], func=AF.Sqrt, scale=1.0 / DI, bias=eps_t[:, 0:1]),
                  reads=(("ssvs",), ("eps",)), writes=(("rtv",),))
            P.add("dve", lambda e: e.reciprocal(out=rstdv[:, :], in_=rtv[:, :]),
                  reads=(("rtv",),), writes=(("rstdv",),))
            unit = 0
            for jp in range(8):
                s = load_wtile()
                W = wslot[s][:, :].rearrange("p (a m k c) -> p a m k c", a=2, m=2, k=KD)
                g = jp
                wb = wsTs[g % 2]
                ws_g = cst[:, C_WS + la * 1024 + g * 128:C_WS + la * 1024 + (g + 1) * 128]
                P.add("dve", lambda e, wb=wb, ws_g=ws_g: e.tensor_tensor(
                    out=wb[:, :, :], in0=ws_g.unsqueeze(1).broadcast_to([128, NTC, 128]),
                    in1=rstdv[:, :].unsqueeze(2).broadcast_to([128, NTC, 128]), op=ALU.mult),
                    reads=(("wsTm", la), ("rstdv",)), writes=(("wsTs", g % 2),))
                bb_g = cst[:, C_BB + la * 1024 + g * 128:C_BB + la * 1024 + (g + 1) * 128]
                for jj in range(2):
                    j = 2 * jp + jj
                    gcol = C_GV + la * 16 + j
                    for tb in range(NTB):
                        bu, bz, bm = next_bank(), next_bank(), next_bank()
                        ts = unit % 2
                        unit += 1
                        for (bk, m) in ((bz, 1), (bu, 0)):
                            for k in range(KD):
                                P.add("pe", lambda e, bk=bk, W=W, jj=jj, m=m, k=k, tb=tb: e.matmul(
                                    banks[bk][:, :], lhsT=W[:, jj, m, k, :], rhs=hT[:, k, tbs(tb)],
                                    start=(k == 0), stop=(k == KD - 1)),
                                    reads=(("w", s), ("hT", k, tb)), writes=(("bank", bk),))
                        for q in range(4):
                            tc = tb * 4 + q
                            P.add("pe", lambda e, bm=bm, q=q, tc=tc, j=j, wb=wb: e.matmul(
                                banks[bm][:, q * 128:(q + 1) * 128], lhsT=v_sb[:, tc, j * 128:(j + 1) * 128],
                                rhs=wb[:, tc, :], start=True, stop=True),
                                reads=(("v", tc, j // 4), ("wsTs", g % 2)), writes=(("bank", bm),))
                        P.add("act", lambda e, bz=bz, ts=ts: e.activation(out=tmp_sz[ts][:, :], in_=banks[bz][:, :], func=AF.Silu),
                              reads=(("bank", bz),), writes=(("tsz", ts),))
                        P.add("dve", lambda e, bu=bu, ts=ts: e.tensor_tensor(
                            out=tmp_sz[ts][:, :], in0=banks[bu][:, :], in1=tmp_sz[ts][:, :], op=ALU.mult),
                            reads=(("bank", bu), ("tsz", ts)), writes=(("tsz", ts),))
                        P.add("dve", lambda e, bm=bm, ts=ts, gcol=gcol, bb_g=bb_g: e.scalar_tensor_tensor(
                            out=tmp_m[ts][:, :].rearrange("p (a c) -> p a c", a=4),
                            in0=banks[bm][:, :].rearrange("p (a c) -> p a c", a=4),
                            scalar=cst[:, gcol:gcol + 1],
                            in1=bb_g.unsqueeze(1).broadcast_to([128, 4, 128]),
                            op0=ALU.mult, op1=ALU.add),
                            reads=(("bank", bm), ("cst",), ("cstb",)), writes=(("tm", ts),))
                        P.add("dve", lambda e, ts=ts, j=j, tb=tb: e.tensor_tensor(
                            out=y_sb[:, j, tbs(tb)], in0=tmp_sz[ts][:, :], in1=tmp_m[ts][:, :], op=ALU.mult),
                            reads=(("tsz", ts), ("tm", ts)), writes=(("y", j, tb),))

        def layer_b(lb, pending):
            ucount = [0]

            def b_unit(j, tb, s):
                W = wslot[s][:, :].rearrange("p (m k c) -> p m k c", m=4, k=KD)
                wc = [C_WC + lb * 48 + tap * 16 + j for tap in range(3)]
                bC, bX, bZ, bB = next_bank(), next_bank(), next_bank(), next_bank()
                ts = ucount[0] % 2
                ucount[0] += 1
                for (bk, m) in ((bC, 1), (bX, 2), (bZ, 3), (bB, 0)):
                    for k in range(KD):
                        P.add("pe", lambda e, bk=bk, W=W, m=m, k=k, tb=tb: e.matmul(
                            banks[bk][:, :], lhsT=W[:, m, k, :], rhs=hT[:, k, tbs(tb)],
                            start=(k == 0), stop=(k == KD - 1)),
                            reads=(("w", s), ("hT", k, tb)), writes=(("bank", bk),))
                P.add("act", lambda e, j=j, ts=ts: e.activation(out=xcb[ts][:, 0:2], in_=halo[:, lb, j, :], func=AF.Copy),
                      reads=(("halo", lb, j),), writes=(("xch", ts),))
                P.add("act", lambda e, bC=bC, ts=ts: e.activation(out=tmp_c[ts][:, :], in_=banks[bC][:, :], func=AF.Copy),
                      reads=(("bank", bC),), writes=(("tm", ts),))
                P.add("dve", lambda e, bX=bX, ts=ts: e.tensor_tensor(
                    out=xcb[ts][:, 2:TB + 2], in0=banks[bX][:, :], in1=tmp_c[ts][:, :], op=ALU.mult),
                    reads=(("bank", bX), ("tm", ts)), writes=(("xc", ts),))
                P.add("act", lambda e, j=j, ts=ts: e.activation(out=halo[:, lb, j, :], in_=xcb[ts][:, TB:TB + 2], func=AF.Copy),
                      reads=(("xc", ts),), writes=(("halo", lb, j),))
                P.add("act", lambda e, ts=ts, c=wc[2]: e.activation(
                    out=tmp_a[ts][:, :], in_=xcb[ts][:, 2:TB + 2], func=AF.Copy, scale=cst[:, c:c + 1]),
                    reads=(("xc", ts), ("cst",)), writes=(("ta", ts),))
                P.add("act", lambda e, bZ=bZ, ts=ts: e.activation(out=tmp_sz[ts][:, :], in_=banks[bZ][:, :], func=AF.Silu),
                      reads=(("bank", bZ),), writes=(("tsz", ts),))
                P.add("dve", lambda e, ts=ts, c=wc[1]: e.scalar_tensor_tensor(
                    out=tmp_a[ts][:, :], in0=xcb[ts][:, 1:TB + 1], scalar=cst[:, c:c + 1], in1=tmp_a[ts][:, :],
                    op0=ALU.mult, op1=ALU.add),
                    reads=(("xc", ts), ("xch", ts), ("ta", ts), ("cst",)), writes=(("ta", ts),))
                P.add("dve", lambda e, ts=ts, c=wc[0]: e.scalar_tensor_tensor(
                    out=tmp_c[ts][:, :], in0=xcb[ts][:, 0:TB], scalar=cst[:, c:c + 1], in1=tmp_a[ts][:, :],
                    op0=ALU.mult, op1=ALU.add),
                    reads=(("xc", ts), ("xch", ts), ("ta", ts), ("cst",)), writes=(("tm", ts),))
                P.add("dve", lambda e, bB=bB, ts=ts: e.tensor_tensor(
                    out=tmp_sz[ts][:, :], in0=banks[bB][:, :], in1=tmp_sz[ts][:, :], op=ALU.mult),
                    reads=(("bank", bB), ("tsz", ts)), writes=(("tsz", ts),))
                P.add("dve", lambda e, ts=ts, j=j, tb=tb: e.tensor_tensor(
                    out=y_sb[:, j, tbs(tb)], in0=tmp_sz[ts][:, :], in1=tmp_c[ts][:, :], op=ALU.mult),
                    reads=(("tsz", ts), ("tm", ts)), writes=(("y", j, tb),))

            slots = []
            for j in range(4):
                slots.append(load_wtile())
                b_unit(j, 0, slots[j])
                if j in pending:
                    pending[j]()
            for j in range(4):
                b_unit(j, 1, slots[j])
            for j in range(4, KI):
                s = load_wtile()
                for tb in range(NTB):
                    b_unit(j, tb, s)

        nl = len(layers)
        assert layers[-1] % 2 == 1, 'final-norm staging reuses v_sb: last layer must be a short-conv layer'
        def acc_and_fin_tb1(l):
            for k in range(KD):
                norm_acc(1, k)
            norm_fin(1, C_GN + l * 8, to_x=False)

        for h in range(NPASS):
            state["tile"] = 0
            state["h"] = h
            l0 = layers[0]
            if h > 0:
                state["h"] = h - 1
                norm_fin(1, C_GF, to_x=True)
                state["h"] = h
                input_x(h, 1)
            norm_fin(0, C_GN + l0 * 8, to_x=False, pe_acc=True)
            for li, l in enumerate(layers):
                if li == 0 and h == 0:
                    pending = {1: (lambda: [norm_acc(1, k) for k in range(KD)]),
                               2: (lambda l=l: norm_fin(1, C_GN + l * 8, to_x=False))}
                elif li == 0:
                    pending = {1: (lambda l=l: norm_fin(1, C_GN + l * 8, to_x=False, pe_acc=True))}
                else:
                    pending = {0: (lambda l=l: norm_fin(1, C_GN + l * 8, to_x=False))}
                if l % 2 == 0:
                    layer_a(l // 2, pending)
                else:
                    layer_b(l // 2, pending)
                if li + 1 < nl:
                    hook = (lambda ln=layers[li + 1]: norm_fin(0, C_GN + ln * 8, to_x=False))
                else:
                    def hook(h=h):
                        norm_fin(0, C_GF, to_x=True)
                        if h + 1 < NPASS:
                            input_x(h + 1, 0)
                out_proj(hook)
        norm_fin(1, C_GF, to_x=True)
        P.add("sp", None, writes=tuple(("v", k, i) for k in range(NTC) for i in range(4)))

        P.resolve()
        keys = P.sem_keys()
        sems = {}
        for i, key in enumerate(keys):
            sems[key] = es.enter_context(nc.semaphore(f"s{i}"))
        block = es.enter_context(nc.Block())

        @block.tensor
        def _(e):
            P.emit("pe", e, sems)

        @block.scalar
        def _(e):
            P.emit("act", e, sems)

        @block.vector
        def _(e):
            P.emit("dve", e, sems)

        @block.gpsimd
        def _(e):
            P.emit("pool", e, sems)

        @block.sync
        def _(e):
            P.emit("sp", e, sems)
    return nc


def _pkc(w):
    k = w.shape[0] // 128
    return np.ascontiguousarray(w.reshape(k, 128, w.shape[1]).transpose(1, 0, 2))


def _o_tiles(w_out):
    wr = _pkc(w_out)
    t = wr.reshape(128, KI, 4, 2, 128)
    t = t.transpose(2, 0, 3, 1, 4)
    return np.ascontiguousarray(t).reshape(4, 128, 4096)


def _a_tiles(w_in, w_out):
    wr = _pkc(w_in)
    v = wr[:, :, DI:2 * DI].reshape(128, KD, 4, 512).transpose(2, 0, 1, 3)
    a1 = np.ascontiguousarray(v).reshape(4, 128, 4096)
    u = wr[:, :, 0:DI].reshape(128, KD, 8, 2, 128)
    z = wr[:, :, 2 * DI:3 * DI].reshape(128, KD, 8, 2, 128)
    uz = np.stack([u, z], axis=0)
    a2 = np.ascontiguousarray(uz.transpose(3, 1, 4, 0, 2, 5)).reshape(8, 128, 4096)
    return np.concatenate([a1, a2, _o_tiles(w_out)], axis=0)


def _b_tiles(w_in, w_out):
    wr = _pkc(w_in)
    t = wr.reshape(128, KD, 4, KI, 128)
    b1 = np.ascontiguousarray(t.transpose(3, 0, 2, 1, 4)).reshape(KI, 128, 4096)
    return np.concatenate([b1, _o_tiles(w_out)], axis=0)


def _consts(norm_g, final_g, a_v_norm_g, a_w_s, a_b_s, b_w_conv):
    c = np.zeros((128, NCST), np.float32)
    c[:, C_GN:C_GN + 32] = norm_g.reshape(4, KD, 128).transpose(2, 0, 1).reshape(128, 32)
    c[:, C_GF:C_GF + 8] = final_g.reshape(KD, 128).T
    c[:, C_GV:C_GV + 32] = a_v_norm_g.reshape(2, KI, 128).transpose(2, 0, 1).reshape(128, 32)
    c[:, C_WC:C_WC + 96] = b_w_conv.reshape(2, 3, KI, 128).transpose(3, 0, 1, 2).reshape(128, 96)
    c[:, C_ID:C_ID + 128] = np.eye(128, dtype=np.float32)
    c[:, C_MK:C_MK + 128] = np.triu(np.ones((128, 128), np.float32))
    c[:, C_WS:C_WS + 2048] = a_w_s.transpose(3, 0, 1, 2).reshape(128, 2048)
    c[:, C_BB:C_BB + 2048] = np.broadcast_to(a_b_s.reshape(1, 2048), (128, 2048))
    return c


LAUNCH_GROUPS = [[0, 1, 2, 3]]
_prog_cache = {}


def kernel(x, norm_g, final_g, a_w_in, a_v_norm_g, a_w_s, a_b_s, a_w_out, b_w_in, b_w_conv, b_w_out):
    f = lambda a: np.asarray(a, dtype=np.float32)
    x = f(x)
    cst = _consts(f(norm_g), f(final_g), f(a_v_norm_g), f(a_w_s), f(a_b_s), f(b_w_conv))
    tiles = {}
    for l in range(4):
        if l % 2 == 0:
            tiles[l] = _a_tiles(f(a_w_in)[l // 2], f(a_w_out)[l // 2])
        else:
            tiles[l] = _b_tiles(f(b_w_in)[l // 2], f(b_w_out)[l // 2])
    cur = [np.ascontiguousarray(x[c].reshape(NPASS, NTB, TB, KD, 128).transpose(4, 0, 1, 3, 2)) for c in range(N_CORES)]
    for gi, group in enumerate(LAUNCH_GROUPS):
        last = gi == len(LAUNCH_GROUPS) - 1
        key = (tuple(group), last)
        if key not in _prog_cache:
            _prog_cache[key] = build_program(group, final_norm=last)
        nc = _prog_cache[key]
        wt = np.concatenate([tiles[l] for l in group], axis=0)
        in_maps = [{"x": cur[c], "wt": wt, "cst": cst} for c in range(N_CORES)]
        res = run_bass_kernel_spmd(nc, in_maps, core_ids=list(range(N_CORES)))
        cur = [np.asarray(res.results[c]["out"], dtype=np.float32) for c in range(N_CORES)]
    o = np.stack(cur, axis=0)
    return np.ascontiguousarray(o.transpose(0, 2, 3, 5, 4, 1)).reshape(N_CORES, SEQ, D)
```

```python
import numpy as np
from contextlib import ExitStack

import concourse.bass as bass
import concourse.mybir as mybir
from concourse.bass_utils import run_bass_kernel_spmd

F32 = mybir.dt.float32
F32R = mybir.dt.float32r
BF16 = mybir.dt.bfloat16
AF = mybir.ActivationFunctionType
ALU = mybir.AluOpType
AX = mybir.AxisListType

D = 1024
DI = 2048
SEQ = 2048
T = 1024
NPASS = SEQ // T
TB = 512
NTB = T // TB
NTC = T // 128
KD = D // 128
KI = DI // 128
EPS = 1e-6
N_CORES = 8

C_GN = 0
C_GF = C_GN + 32
C_GV = C_GF + 8
C_WC = C_GV + 32
C_ID = C_WC + 96
C_MK = C_ID + 128
C_WS = C_MK + 128
C_BB = C_WS + 2048
NCST = C_BB + 2048

NSLOT = 4
TILES_A = 16
TILES_B = 20


class Op:
    __slots__ = ("eng", "fn", "reads", "writes", "dsem", "deps", "signal", "value")

    def __init__(self, eng, fn, reads, writes, dsem):
        self.eng = eng
        self.fn = fn
        self.reads = reads
        self.writes = writes
        self.dsem = dsem
        self.deps = ()
        self.signal = dsem is not None
        self.value = 0


class Prog:
    ENGS = ("pe", "act", "dve", "pool", "sp")

    def __init__(self):
        self.ops = []

    def add(self, eng, fn, reads=(), writes=(), dsem=None):
        self.ops.append(Op(eng, fn, tuple(reads), tuple(writes), dsem))

    def resolve(self):
        last_writer = {}
        readers = {}
        ops = self.ops
        for i, op in enumerate(ops):
            deps = set()
            for r in op.reads:
                w = last_writer.get(r)
                if w is not None:
                    deps.add(w)
            for r in op.writes:
                w = last_writer.get(r)
                if w is not None:
                    deps.add(w)
                for rd in readers.get(r, ()):
                    deps.add(rd)
            deps.discard(i)
            if op.eng == "pe":
                deps = {d for d in deps if ops[d].eng != "pe"}
            best = {}
            for d in deps:
                key = ops[d].dsem if ops[d].dsem is not None else ops[d].eng
                if d > best.get(key, -1):
                    best[key] = d
            op.deps = tuple(sorted(best.values()))
            for d in op.deps:
                ops[d].signal = True
            for r in op.reads:
                readers.setdefault(r, []).append(i)
            for r in op.writes:
                last_writer[r] = i
                readers[r] = []
        counters = {}
        for op in ops:
            if op.signal:
                key = op.dsem if op.dsem is not None else op.eng
                inc = 16 if op.dsem is not None else 1
                counters[key] = counters.get(key, 0) + inc
                op.value = counters[key]
        return counters

    def sem_keys(self):
        keys = []
        for op in self.ops:
            if op.signal:
                key = op.dsem if op.dsem is not None else op.eng
                if key not in keys:
                    keys.append(key)
        return keys

    def emit(self, engname, e, sems):
        ops = self.ops
        seen = {}
        for op in ops:
            if op.eng != engname:
                continue
            need = {}
            for d in op.deps:
                dop = ops[d]
                key = dop.dsem if dop.dsem is not None else dop.eng
                if dop.value > need.get(key, 0):
                    need[key] = dop.value
            for key, val in need.items():
                if seen.get(key, 0) < val:
                    e.wait_ge(sems[key], val)
                    seen[key] = val
            if op.fn is None:
                continue
            ins = op.fn(e)
            if op.signal:
                key = op.dsem if op.dsem is not None else op.eng
                ins.then_inc(sems[key], 16 if op.dsem is not None else 1)


def build_program(layers, final_norm):
    n_tiles = sum(TILES_A if l % 2 == 0 else TILES_B for l in layers)
    nc = bass.Bass("TRN2", target_bir_lowering=False)
    x_d = nc.dram_tensor("x", [128, NPASS, NTB, KD, TB], F32, kind="ExternalInput").ap()
    wt_d = nc.dram_tensor("wt", [n_tiles, 128, 4096], F32, kind="ExternalInput").ap()
    cst_d = nc.dram_tensor("cst", [128, NCST], F32, kind="ExternalInput").ap()
    out_d = nc.dram_tensor("out", [128, NPASS, NTB, KD, TB], F32, kind="ExternalOutput").ap()

    P = Prog()
    es = ExitStack()
    with es:
        es.enter_context(nc.allow_low_precision("bf16 matmul operands, fp32 accumulation"))

        def sb(name, shape, dt):
            return es.enter_context(nc.sbuf_tensor(name, shape, dt))

        x_sb = sb("x_sb", [128, KD, T], F32)
        hT = sb("hT", [128, KD, T], BF16)
        y_sb = sb("y_sb", [128, KI, T], BF16)
        v_sb = sb("v_sb", [128, NTC, DI], BF16)
        wslot = [sb(f"wslot{i}", [128, 4096], BF16) for i in range(NSLOT)]
        cst = sb("cst_sb", [128, NCST], F32)
        eps_t = sb("eps_t", [128, 1], F32)
        warm_t = sb("warm_t", [128, 1], F32)
        sq = [sb(f"sq{i}", [128, TB], F32) for i in range(3)]
        acc = [sb(f"acc{i}", [128, TB], F32) for i in range(2)]
        ones_f = sb("ones_f", [128, 128], F32)
        ones_bf = sb("ones_bf", [128, 128], BF16)
        sqb = [sb(f"sqb{i}", [128, TB], BF16) for i in range(4)]
        rstd = [sb(f"rstd{i}", [128, TB], F32) for i in range(2)]
        rt = rstd
        junk = [sb(f"junk{i}", [128, TB], BF16) for i in range(2)]
        ssv = sb("ssv", [128, NTC * 4], F32)
        ssvs = sb("ssvs", [128, NTC], F32)
        rtv = sb("rtv", [128, NTC], F32)
        rstdv = sb("rstdv", [128, NTC], F32)
        wsTs = [sb(f"wsTs{i}", [128, NTC, 128], BF16) for i in range(2)]
        tmp_sz = [sb(f"tsz{i}", [128, TB], F32) for i in range(2)]
        tmp_m = [sb(f"tm{i}", [128, TB], F32) for i in range(2)]
        tmp_c = tmp_m
        tmp_a = [sb(f"ta{i}", [128, TB], F32) for i in range(2)]
        xcb = [sb(f"xc{i}", [128, TB + 2], F32) for i in range(2)]
        halo = sb("halo", [128, 2, KI, 2], F32)
        banks = [es.enter_context(nc.psum_tensor(f"bank{i}", [128, 512], F32)) for i in range(8)]

        state = {"bank": 0, "tile": 0, "slot": 0, "sq": 0, "sqb": 0, "nload": 0}

        def next_bank():
            b = state["bank"]
            state["bank"] = (b + 1) % 8
            return b

        def tbs(tb):
            return slice(tb * TB, (tb + 1) * TB)

        def load_wtile(extra_reads=()):
            s = state["slot"]
            state["slot"] = (s + 1) % NSLOT
            idx = state["tile"]
            state["tile"] += 1
            extra = tuple(extra_reads)
            first = state["nload"] < NSLOT
            if idx == 0 and not first:
                extra = extra + (("x", 0, 0),)
            if first:
                extra = extra + ((("w", s - 1),) if state["nload"] > 0 else (("x", 0, 0),))
            state["nload"] += 1
            P.add("pool", lambda e, s=s, idx=idx: e.dma_start(out=wslot[s][:, :], in_=wt_d[idx]),
                  reads=extra, writes=(("w", s),), dsem=("w", s))
            if state["nload"] == 1:
                input_x(0, 1, extra_reads=(("w", s),))
            if state["nload"] == NSLOT:
                load_big_consts()
            return s

        yo = v_sb[:, :, :].bitcast(F32)

        def input_x(h, tb, extra_reads=()):
            P.add("sp", lambda e, tb=tb, h=h: e.dma_start(out=x_sb[:, :, tbs(tb)], in_=x_d[:, h, tb, :, :]),
                  reads=tuple(extra_reads), writes=tuple(("x", k, tb) for k in range(KD)), dsem=("ld", tb))

        def output_y(h, tb, kh):
            k0 = kh * 4
            P.add("sp", lambda e, tb=tb, h=h, k0=k0: e.dma_start(out=out_d[:, h, tb, k0:k0 + 4, :], in_=yo[:, k0:k0 + 4, tbs(tb)]),
                  reads=tuple(("v", k, 2 * tb + i) for k in range(k0, k0 + 4) for i in range(2)), dsem=("st", tb, kh))

        P.add("sp", lambda e: e.dma_start(out=cst[:, 0:C_WS], in_=cst_d[:, 0:C_WS]), writes=(("cst",),), dsem=("cst",))
        input_x(0, 0)
        P.add("dve", lambda e: e.memset(acc[0][:, 0:128], 1.0), writes=(("acc", 0),))
        P.add("dve", lambda e: e.tensor_copy(out=ones_f[:, :].bitcast(F32R), in_=acc[0][:, 0:128]), reads=(("acc", 0),), writes=(("ones",),))
        P.add("dve", lambda e: e.memset(ones_bf[:, :], 1.0), writes=(("onesb",),))
        P.add("dve", lambda e: e.memset(eps_t[:, :], EPS), writes=(("eps",),))
        P.add("act", lambda e: e.activation(out=warm_t[:, :], in_=eps_t[:, :], func=AF.Square), reads=(("eps",),), writes=(("warm",),))
        P.add("dve", lambda e: e.memset(halo[:, :, :, :], 0.0),
              writes=tuple(("halo", lb, j) for lb in range(2) for j in range(KI)))

        def load_big_consts():
            P.add("sp", lambda e: e.dma_start(out=cst[:, C_WS:NCST], in_=cst_d[:, C_WS:NCST]),
                  reads=(("w", NSLOT - 1),), writes=(("cstb",),), dsem=("cstb",))
            for la in range(2):
                if 2 * la in layers:
                    ws_ap = cst[:, C_WS + la * 1024:C_WS + (la + 1) * 1024].rearrange("p (g t) -> p g t", g=8)
                    mk_ap = cst[:, C_MK:C_MK + 128].unsqueeze(1).broadcast_to([128, 8, 128])
                    P.add("dve", lambda e, ws_ap=ws_ap, mk_ap=mk_ap: e.tensor_tensor(out=ws_ap, in0=ws_ap, in1=mk_ap, op=ALU.mult),
                          reads=(("cst",), ("cstb",)), writes=(("wsTm", la),))

        def norm_acc(tb, k):
            if k == 0:
                P.add("act", lambda e, tb=tb: e.activation(out=acc[tb][:, :].bitcast(F32R), in_=x_sb[:, 0, tbs(tb)], func=AF.Square),
                      reads=(("x", 0, tb),), writes=(("acc", tb),))
                return
            i = state["sq"] % 3
            state["sq"] += 1
            P.add("act", lambda e, i=i, k=k, tb=tb: e.activation(out=sq[i][:, :], in_=x_sb[:, k, tbs(tb)], func=AF.Square),
                  reads=(("x", k, tb),), writes=(("sq", i),))
            oap = (lambda tb=tb: acc[tb][:, :].bitcast(F32R))
            P.add("dve", lambda e, i=i, tb=tb, oap=oap: e.tensor_tensor(out=oap(), in0=acc[tb][:, :], in1=sq[i][:, :], op=ALU.add),
                  reads=(("acc", tb), ("sq", i)), writes=(("acc", tb),))

        def norm_fin(tb, gcol, to_x, pe_acc=False):
            b = next_bank()
            if pe_acc:
                for k in range(KD):
                    i = state["sqb"] % 4
                    state["sqb"] += 1
                    P.add("act", lambda e, i=i, k=k, tb=tb: e.activation(out=sqb[i][:, :], in_=x_sb[:, k, tbs(tb)], func=AF.Square),
                          reads=(("x", k, tb),), writes=(("sqb", i),))
                    P.add("pe", lambda e, b=b, i=i, k=k: e.matmul(banks[b][:, :], lhsT=ones_bf[:, :], rhs=sqb[i][:, :],
                                                                  start=(k == 0), stop=(k == KD - 1)),
                          reads=(("sqb", i), ("onesb",)), writes=(("bank", b),))
            else:
                P.add("pe", lambda e, b=b, tb=tb: e.matmul(banks[b][:, :], lhsT=ones_f[:, :].bitcast(F32R), rhs=acc[tb][:, :].bitcast(F32R), start=True, stop=True),
                      reads=(("acc", tb), ("ones",)), writes=(("bank", b),))
            P.add("act", lambda e, b=b, tb=tb: e.activation(out=rt[tb][:, :], in_=banks[b][:, :], func=AF.Sqrt,
                                                            scale=1.0 / D, bias=eps_t[:, 0:1]),
                  reads=(("bank", b), ("eps",)), writes=(("rstd", tb),))
            P.add("dve", lambda e, tb=tb: e.reciprocal(out=rstd[tb][:, :], in_=rt[tb][:, :]),
                  reads=(("rstd", tb),), writes=(("rstd", tb),))
            for k in range(KD):
                if to_x:
                    out_fn = lambda k=k, tb=tb: yo[:, k, tbs(tb)]
                    wr = (("v", k, 2 * tb), ("v", k, 2 * tb + 1))
                else:
                    out_fn = lambda k=k, tb=tb: hT[:, k, tbs(tb)]
                    wr = (("hT", k, tb),)
                P.add("dve", lambda e, k=k, tb=tb, out_fn=out_fn: e.scalar_tensor_tensor(
                    out=out_fn(), in0=x_sb[:, k, tbs(tb)], scalar=cst[:, gcol + k:gcol + k + 1],
                    in1=rstd[tb][:, :], op0=ALU.mult, op1=ALU.mult),
                    reads=(("x", k, tb), ("rstd", tb), ("cst",)), writes=wr)
                if to_x and k % 4 == 3:
                    output_y(state["h"], tb, k // 4)

        def out_proj(hook):
            slots = []
            for tb in range(NTB):
                for dp in range(4):
                    if tb == 0:
                        slots.append(load_wtile())
                    s = slots[dp]
                    W = wslot[s][:, :].rearrange("p (d k c) -> p d k c", d=2, k=KI)
                    for dd in range(2):
                        dmc = 2 * dp + dd
                        b = next_bank()
                        for k in range(KI):
                            P.add("pe", lambda e, b=b, W=W, dd=dd, k=k, tb=tb: e.matmul(
                                banks[b][:, :], lhsT=W[:, dd, k, :], rhs=y_sb[:, k, tbs(tb)],
                                start=(k == 0), stop=(k == KI - 1)),
                                reads=(("w", s), ("y", k, tb)), writes=(("bank", b),))
                        P.add("dve", lambda e, b=b, dmc=dmc, tb=tb: e.tensor_tensor(
                            out=x_sb[:, dmc, tbs(tb)], in0=banks[b][:, :], in1=x_sb[:, dmc, tbs(tb)], op=ALU.add),
                            reads=(("bank", b), ("x", dmc, tb)), writes=(("x", dmc, tb),))
                        norm_acc(tb, dmc)
                    if tb == 1 and dp == 0 and hook is not None:
                        hook()

        def layer_a(la, pending):
            slots = []
            for half in range(2):
                for nb in range(4):
                    if half == 0:
                        slots.append(load_wtile())
                    s = slots[nb]
                    W = wslot[s][:, :].rearrange("p (k c) -> p k c", k=KD)
                    for tc in range(half * 4, half * 4 + 4):
                        b = next_bank()
                        for k in range(KD):
                            P.add("pe", lambda e, b=b, W=W, k=k, tc=tc: e.matmul(
                                banks[b][:, :], lhsT=hT[:, k, tc * 128:(tc + 1) * 128], rhs=W[:, k, :],
                                start=(k == 0), stop=(k == KD - 1)),
                                reads=(("w", s), ("hT", k, tc // 4)), writes=(("bank", b),))
                        col = tc * 4 + nb
                        ji = col % 2
                        P.add("act", lambda e, b=b, col=col, ji=ji: e.activation(out=junk[ji][:, :], in_=banks[b][:, :], func=AF.Square,
                                                                                 accum_out=ssv[:, col:col + 1]),
                              reads=(("bank", b),), writes=(("ssv", col), ("junk", ji)))
                        P.add("act", lambda e, b=b, tc=tc, nb=nb: e.activation(out=v_sb[:, tc, nb * 512:(nb + 1) * 512], in_=banks[b][:, :], func=AF.Copy),
                              reads=(("bank", b),), writes=(("v", tc, nb),))
                    if half == 0 and nb in pending:
                        pending[nb]()
            P.add("dve", lambda e: e.tensor_reduce(out=ssvs[:, :], in_=ssv[:, :].rearrange("p (a c) -> p a c", c=4),
                                                   axis=AX.X, op=ALU.add),
                  reads=tuple(("ssv", c) for c in range(NTC * 4)), writes=(("ssvs",),))
            P.add("act", lambda e: e.activation(out=rtv[:, :], in_=ssvs[:, :], func=AF.Sqrt, scale=1.0 / DI, bias=eps_t[:, 0:1]),
                  reads=(("ssvs",), ("eps",)), writes=(("rtv",),))
            P.add("dve", lambda e: e.reciprocal(out=rstdv[:, :], in_=rtv[:, :]),
                  reads=(("rtv",),), writes=(("rstdv",),))
            unit = 0
            for jp in range(8):
                s = load_wtile()
                W = wslot[s][:, :].rearrange("p (a m k c) -> p a m k c", a=2, m=2, k=KD)
                g = jp
                wb = wsTs[g % 2]
                ws_g = cst[:, C_WS + la * 1024 + g * 128:C_WS + la * 1024 + (g + 1) * 128]
                P.add("dve", lambda e, wb=wb, ws_g=ws_g: e.tensor_tensor(
                    out=wb[:, :, :], in0=ws_g.unsqueeze(1).broadcast_to([128, NTC, 128]),
                    in1=rstdv[:, :].unsqueeze(2).broadcast_to([128, NTC, 128]), op=ALU.mult),
                    reads=(("wsTm", la), ("rstdv",)), writes=(("wsTs", g % 2),))
                bb_g = cst[:, C_BB + la * 1024 + g * 128:C_BB + la * 1024 + (g + 1) * 128]
                for jj in range(2):
                    j = 2 * jp + jj
                    gcol = C_GV + la * 16 + j
                    for tb in range(NTB):
                        bu, bz, bm = next_bank(), next_bank(), next_bank()
                        ts = unit % 2
                        unit += 1
                        for (bk, m) in ((bz, 1), (bu, 0)):
                            for k in range(KD):
                                P.add("pe", lambda e, bk=bk, W=W, jj=jj, m=m, k=k, tb=tb: e.matmul(
                                    banks[bk][:, :], lhsT=W[:, jj, m, k, :], rhs=hT[:, k, tbs(tb)],
                                    start=(k == 0), stop=(k == KD - 1)),
                                    reads=(("w", s), ("hT", k, tb)), writes=(("bank", bk),))
                        for q in range(4):
                            tc = tb * 4 + q
                            P.add("pe", lambda e, bm=bm, q=q, tc=tc, j=j, wb=wb: e.matmul(
                                banks[bm][:, q * 128:(q + 1) * 128], lhsT=v_sb[:, tc, j * 128:(j + 1) * 128],
                                rhs=wb[:, tc, :], start=True, stop=True),
                                reads=(("v", tc, j // 4), ("wsTs", g % 2)), writes=(("bank", bm),))
                        P.add("act", lambda e, bz=bz, ts=ts: e.activation(out=tmp_sz[ts][:, :], in_=banks[bz][:, :], func=AF.Silu),
                              reads=(("bank", bz),), writes=(("tsz", ts),))
                        P.add("dve", lambda e, bu=bu, ts=ts: e.tensor_tensor(
                            out=tmp_sz[ts][:, :], in0=banks[bu][:, :], in1=tmp_sz[ts][:, :], op=ALU.mult),
                            reads=(("bank", bu), ("tsz", ts)), writes=(("tsz", ts),))
                        P.add("dve", lambda e, bm=bm, ts=ts, gcol=gcol, bb_g=bb_g: e.scalar_tensor_tensor(
                            out=tmp_m[ts][:, :].rearrange("p (a c) -> p a c", a=4),
                            in0=banks[bm][:, :].rearrange("p (a c) -> p a c", a=4),
                            scalar=cst[:, gcol:gcol + 1],
                            in1=bb_g.unsqueeze(1).broadcast_to([128, 4, 128]),
                            op0=ALU.mult, op1=ALU.add),
                            reads=(("bank", bm), ("cst",), ("cstb",)), writes=(("tm", ts),))
                        P.add("dve", lambda e, ts=ts, j=j, tb=tb: e.tensor_tensor(
                            out=y_sb[:, j, tbs(tb)], in0=tmp_sz[ts][:, :], in1=tmp_m[ts][:, :], op=ALU.mult),
                            reads=(("tsz", ts), ("tm", ts)), writes=(("y", j, tb),))

        def layer_b(lb, pending):
            ucount = [0]

            def b_unit(j, tb, s):
                W = wslot[s][:, :].rearrange("p (m k c) -> p m k c", m=4, k=KD)
                wc = [C_WC + lb * 48 + tap * 16 + j for tap in range(3)]
                bC, bX, bZ, bB = next_bank(), next_bank(), next_bank(), next_bank()
                ts = ucount[0] % 2
                ucount[0] += 1
                for (bk, m) in ((bC, 1), (bX, 2), (bZ, 3), (bB, 0)):
                    for k in range(KD):
                        P.add("pe", lambda e, bk=bk, W=W, m=m, k=k, tb=tb: e.matmul(
                            banks[bk][:, :], lhsT=W[:, m, k, :], rhs=hT[:, k, tbs(tb)],
                            start=(k == 0), stop=(k == KD - 1)),
                            reads=(("w", s), ("hT", k, tb)), writes=(("bank", bk),))
                P.add("act", lambda e, j=j, ts=ts: e.activation(out=xcb[ts][:, 0:2], in_=halo[:, lb, j, :], func=AF.Copy),
                      reads=(("halo", lb, j),), writes=(("xch", ts),))
                P.add("act", lambda e, bC=bC, ts=ts: e.activation(out=tmp_c[ts][:, :], in_=banks[bC][:, :], func=AF.Copy),
                      reads=(("bank", bC),), writes=(("tm", ts),))
                P.add("dve", lambda e, bX=bX, ts=ts: e.tensor_tensor(
                    out=xcb[ts][:, 2:TB + 2], in0=banks[bX][:, :], in1=tmp_c[ts][:, :], op=ALU.mult),
                    reads=(("bank", bX), ("tm", ts)), writes=(("xc", ts),))
                P.add("act", lambda e, j=j, ts=ts: e.activation(out=halo[:, lb, j, :], in_=xcb[ts][:, TB:TB + 2], func=AF.Copy),
                      reads=(("xc", ts),), writes=(("halo", lb, j),))
                P.add("act", lambda e, ts=ts, c=wc[2]: e.activation(
                    out=tmp_a[ts][:, :], in_=xcb[ts][:, 2:TB + 2], func=AF.Copy, scale=cst[:, c:c + 1]),
                    reads=(("xc", ts), ("cst",)), writes=(("ta", ts),))
                P.add("act", lambda e, bZ=bZ, ts=ts: e.activation(out=tmp_sz[ts][:, :], in_=banks[bZ][:, :], func=AF.Silu),
                      reads=(("bank", bZ),), writes=(("tsz", ts),))
                P.add("dve", lambda e, ts=ts, c=wc[1]: e.scalar_tensor_tensor(
                    out=tmp_a[ts][:, :], in0=xcb[ts][:, 1:TB + 1], scalar=cst[:, c:c + 1], in1=tmp_a[ts][:, :],
                    op0=ALU.mult, op1=ALU.add),
                    reads=(("xc", ts), ("xch", ts), ("ta", ts), ("cst",)), writes=(("ta", ts),))
                P.add("dve", lambda e, ts=ts, c=wc[0]: e.scalar_tensor_tensor(
                    out=tmp_c[ts][:, :], in0=xcb[ts][:, 0:TB], scalar=cst[:, c:c + 1], in1=tmp_a[ts][:, :],
                    op0=ALU.mult, op1=ALU.add),
                    reads=(("xc", ts), ("xch", ts), ("ta", ts), ("cst",)), writes=(("tm", ts),))
                P.add("dve", lambda e, bB=bB, ts=ts: e.tensor_tensor(
                    out=tmp_sz[ts][:, :], in0=banks[bB][:, :], in1=tmp_sz[ts][:, :], op=ALU.mult),
                    reads=(("bank", bB), ("tsz", ts)), writes=(("tsz", ts),))
                P.add("dve", lambda e, ts=ts, j=j, tb=tb: e.tensor_tensor(
                    out=y_sb[:, j, tbs(tb)], in0=tmp_sz[ts][:, :], in1=tmp_c[ts][:, :], op=ALU.mult),
                    reads=(("tsz", ts), ("tm", ts)), writes=(("y", j, tb),))

            slots = []
            for j in range(4):
                slots.append(load_wtile())
                b_unit(j, 0, slots[j])
                if j in pending:
                    pending[j]()
            for j in range(4):
                b_unit(j, 1, slots[j])
            for j in range(4, KI):
                s = load_wtile()
                for tb in range(NTB):
                    b_unit(j, tb, s)

        nl = len(layers)
        assert layers[-1] % 2 == 1, 'final-norm staging reuses v_sb: last layer must be a short-conv layer'
        def acc_and_fin_tb1(l):
            for k in range(KD):
                norm_acc(1, k)
            norm_fin(1, C_GN + l * 8, to_x=False)

        for h in range(NPASS):
            state["tile"] = 0
            state["h"] = h
            l0 = layers[0]
            if h > 0:
                state["h"] = h - 1
                norm_fin(1, C_GF, to_x=True)
                state["h"] = h
                input_x(h, 1)
            norm_fin(0, C_GN + l0 * 8, to_x=False, pe_acc=True)
            for li, l in enumerate(layers):
                if li == 0 and h == 0:
                    pending = {1: (lambda: [norm_acc(1, k) for k in range(KD)]),
                               2: (lambda l=l: norm_fin(1, C_GN + l * 8, to_x=False))}
                elif li == 0:
                    pending = {1: (lambda l=l: norm_fin(1, C_GN + l * 8, to_x=False, pe_acc=True))}
                else:
                    pending = {0: (lambda l=l: norm_fin(1, C_GN + l * 8, to_x=False))}
                if l % 2 == 0:
                    layer_a(l // 2, pending)
                else:
                    layer_b(l // 2, pending)
                if li + 1 < nl:
                    hook = (lambda ln=layers[li + 1]: norm_fin(0, C_GN + ln * 8, to_x=False))
                else:
                    def hook(h=h):
                        norm_fin(0, C_GF, to_x=True)
                        if h + 1 < NPASS:
                            input_x(h + 1, 0)
                out_proj(hook)
        norm_fin(1, C_GF, to_x=True)
        P.add("sp", None, writes=tuple(("v", k, i) for k in range(NTC) for i in range(4)))

        P.resolve()
        keys = P.sem_keys()
        sems = {}
        for i, key in enumerate(keys):
            sems[key] = es.enter_context(nc.semaphore(f"s{i}"))
        block = es.enter_context(nc.Block())

        @block.tensor
        def _(e):
            P.emit("pe", e, sems)

        @block.scalar
        def _(e):
            P.emit("act", e, sems)

        @block.vector
        def _(e):
            P.emit("dve", e, sems)

        @block.gpsimd
        def _(e):
            P.emit("pool", e, sems)

        @block.sync
        def _(e):
            P.emit("sp", e, sems)
    return nc


def _pkc(w):
    k = w.shape[0] // 128
    return np.ascontiguousarray(w.reshape(k, 128, w.shape[1]).transpose(1, 0, 2))


def _o_tiles(w_out):
    wr = _pkc(w_out)
    t = wr.reshape(128, KI, 4, 2, 128)
    t = t.transpose(2, 0, 3, 1, 4)
    return np.ascontiguousarray(t).reshape(4, 128, 4096)


def _a_tiles(w_in, w_out):
    wr = _pkc(w_in)
    v = wr[:, :, DI:2 * DI].reshape(128, KD, 4, 512).transpose(2, 0, 1, 3)
    a1 = np.ascontiguousarray(v).reshape(4, 128, 4096)
    u = wr[:, :, 0:DI].reshape(128, KD, 8, 2, 128)
    z = wr[:, :, 2 * DI:3 * DI].reshape(128, KD, 8, 2, 128)
    uz = np.stack([u, z], axis=0)
    a2 = np.ascontiguousarray(uz.transpose(3, 1, 4, 0, 2, 5)).reshape(8, 128, 4096)
    return np.concatenate([a1, a2, _o_tiles(w_out)], axis=0)


def _b_tiles(w_in, w_out):
    wr = _pkc(w_in)
    t = wr.reshape(128, KD, 4, KI, 128)
    b1 = np.ascontiguousarray(t.transpose(3, 0, 2, 1, 4)).reshape(KI, 128, 4096)
    return np.concatenate([b1, _o_tiles(w_out)], axis=0)


def _consts(norm_g, final_g, a_v_norm_g, a_w_s, a_b_s, b_w_conv):
    c = np.zeros((128, NCST), np.float32)
    c[:, C_GN:C_GN + 32] = norm_g.reshape(4, KD, 128).transpose(2, 0, 1).reshape(128, 32)
    c[:, C_GF:C_GF + 8] = final_g.reshape(KD, 128).T
    c[:, C_GV:C_GV + 32] = a_v_norm_g.reshape(2, KI, 128).transpose(2, 0, 1).reshape(128, 32)
    c[:, C_WC:C_WC + 96] = b_w_conv.reshape(2, 3, KI, 128).transpose(3, 0, 1, 2).reshape(128, 96)
    c[:, C_ID:C_ID + 128] = np.eye(128, dtype=np.float32)
    c[:, C_MK:C_MK + 128] = np.triu(np.ones((128, 128), np.float32))
    c[:, C_WS:C_WS + 2048] = a_w_s.transpose(3, 0, 1, 2).reshape(128, 2048)
    c[:, C_BB:C_BB + 2048] = np.broadcast_to(a_b_s.reshape(1, 2048), (128, 2048))
    return c


LAUNCH_GROUPS = [[0, 1, 2, 3]]
_prog_cache = {}


def kernel(x, norm_g, final_g, a_w_in, a_v_norm_g, a_w_s, a_b_s, a_w_out, b_w_in, b_w_conv, b_w_out):
    f = lambda a: np.asarray(a, dtype=np.float32)
    x = f(x)
    cst = _consts(f(norm_g), f(final_g), f(a_v_norm_g), f(a_w_s), f(a_b_s), f(b_w_conv))
    tiles = {}
    for l in range(4):
        if l % 2 == 0:
            tiles[l] = _a_tiles(f(a_w_in)[l // 2], f(a_w_out)[l // 2])
        else:
            tiles[l] = _b_tiles(f(b_w_in)[l // 2], f(b_w_out)[l // 2])
    cur = [np.ascontiguousarray(x[c].reshape(NPASS, NTB, TB, KD, 128).transpose(4, 0, 1, 3, 2)) for c in range(N_CORES)]
    for gi, group in enumerate(LAUNCH_GROUPS):
        last = gi == len(LAUNCH_GROUPS) - 1
        key = (tuple(group), last)
        if key not in _prog_cache:
            _prog_cache[key] = build_program(group, final_norm=last)
        nc = _prog_cache[key]
        wt = np.concatenate([tiles[l] for l in group], axis=0)
        in_maps = [{"x": cur[c], "wt": wt, "cst": cst} for c in range(N_CORES)]
        res = run_bass_kernel_spmd(nc, in_maps, core_ids=list(range(N_CORES)))
        cur = [np.asarray(res.results[c]["out"], dtype=np.float32) for c in range(N_CORES)]
    o = np.stack(cur, axis=0)
    return np.ascontiguousarray(o.transpose(0, 2, 3, 5, 4, 1)).reshape(N_CORES, SEQ, D)
```

```python
import numpy as np
from contextlib import ExitStack

import concourse.bass as bass
import concourse.mybir as mybir
from concourse.bass_utils import run_bass_kernel_spmd

F32 = mybir.dt.float32
F32R = mybir.dt.float32r
BF16 = mybir.dt.bfloat16
AF = mybir.ActivationFunctionType
ALU = mybir.AluOpType
AX = mybir.AxisListType

D = 1024
DI = 2048
SEQ = 2048
T = 1024
NPASS = SEQ // T
TB = 512
NTB = T // TB
NTC = T // 128
KD = D // 128
KI = DI // 128
EPS = 1e-6
N_CORES = 8

C_GN = 0
C_GF = C_GN + 32
C_GV = C_GF + 8
C_WC = C_GV + 32
C_ID = C_WC + 96
C_MK = C_ID + 128
C_WS = C_MK + 128
C_BB = C_WS + 2048
NCST = C_BB + 2048

NSLOT = 4
TILES_A = 16
TILES_B = 20


class Op:
    __slots__ = ("eng", "fn", "reads", "writes", "dsem", "deps", "signal", "value")

    def __init__(self, eng, fn, reads, writes, dsem):
        self.eng = eng
        self.fn = fn
        self.reads = reads
        self.writes = writes
        self.dsem = dsem
        self.deps = ()
        self.signal = dsem is not None
        self.value = 0


class Prog:
    ENGS = ("pe", "act", "dve", "pool", "sp")

    def __init__(self):
        self.ops = []

    def add(self, eng, fn, reads=(), writes=(), dsem=None):
        self.ops.append(Op(eng, fn, tuple(reads), tuple(writes), dsem))

    def resolve(self):
        last_writer = {}
        readers = {}
        ops = self.ops
        for i, op in enumerate(ops):
            deps = set()
            for r in op.reads:
                w = last_writer.get(r)
                if w is not None:
                    deps.add(w)
            for r in op.writes:
                w = last_writer.get(r)
                if w is not None:
                    deps.add(w)
                for rd in readers.get(r, ()):
                    deps.add(rd)
            deps.discard(i)
            if op.eng == "pe":
                deps = {d for d in deps if ops[d].eng != "pe"}
            best = {}
            for d in deps:
                key = ops[d].dsem if ops[d].dsem is not None else ops[d].eng
                if d > best.get(key, -1):
                    best[key] = d
            op.deps = tuple(sorted(best.values()))
            for d in op.deps:
                ops[d].signal = True
            for r in op.reads:
                readers.setdefault(r, []).append(i)
            for r in op.writes:
                last_writer[r] = i
                readers[r] = []
        counters = {}
        for op in ops:
            if op.signal:
                key = op.dsem if op.dsem is not None else op.eng
                inc = 16 if op.dsem is not None else 1
                counters[key] = counters.get(key, 0) + inc
                op.value = counters[key]
        return counters

    def sem_keys(self):
        keys = []
        for op in self.ops:
            if op.signal:
                key = op.dsem if op.dsem is not None else op.eng
                if key not in keys:
                    keys.append(key)
        return keys

    def emit(self, engname, e, sems):
        ops = self.ops
        seen = {}
        for op in ops:
            if op.eng != engname:
                continue
            need = {}
            for d in op.deps:
                dop = ops[d]
                key = dop.dsem if dop.dsem is not None else dop.eng
                if dop.value > need.get(key, 0):
                    need[key] = dop.value
            for key, val in need.items():
                if seen.get(key, 0) < val:
                    e.wait_ge(sems[key], val)
                    seen[key] = val
            if op.fn is None:
                continue
            ins = op.fn(e)
            if op.signal:
                key = op.dsem if op.dsem is not None else op.eng
                ins.then_inc(sems[key], 16 if op.dsem is not None else 1)


def build_program(layers, final_norm):
    n_tiles = sum(TILES_A if l % 2 == 0 else TILES_B for l in layers)
    nc = bass.Bass("TRN2", target_bir_lowering=False)
    x_d = nc.dram_tensor("x", [128, NPASS, NTB, KD, TB], F32, kind="ExternalInput").ap()
    wt_d = nc.dram_tensor("wt", [n_tiles, 128, 4096], F32, kind="ExternalInput").ap()
    cst_d = nc.dram_tensor("cst", [128, NCST], F32, kind="ExternalInput").ap()
    out_d = nc.dram_tensor("out", [128, NPASS, NTB, KD, TB], F32, kind="ExternalOutput").ap()

    P = Prog()
    es = ExitStack()
    with es:
        es.enter_context(nc.allow_low_precision("bf16 matmul operands, fp32 accumulation"))

        def sb(name, shape, dt):
            return es.enter_context(nc.sbuf_tensor(name, shape, dt))

        x_sb = sb("x_sb", [128, KD, T], F32)
        hT = sb("hT", [128, KD, T], BF16)
        y_sb = sb("y_sb", [128, KI, T], BF16)
        v_sb = sb("v_sb", [128, NTC, DI], BF16)
        wslot = [sb(f"wslot{i}", [128, 4096], BF16) for i in range(NSLOT)]
        cst = sb("cst_sb", [128, NCST], F32)
        eps_t = sb("eps_t", [128, 1], F32)
        warm_t = sb("warm_t", [128, 1], F32)
        sq = [sb(f"sq{i}", [128, TB], F32) for i in range(3)]
        acc = [sb(f"acc{i}", [128, TB], F32) for i in range(2)]
        ones_f = sb("ones_f", [128, 128], F32)
        ones_bf = sb("ones_bf", [128, 128], BF16)
        sqb = [sb(f"sqb{i}", [128, TB], BF16) for i in range(4)]
        rstd = [sb(f"rstd{i}", [128, TB], F32) for i in range(2)]
        rt = rstd
        junk = [sb(f"junk{i}", [128, TB], BF16) for i in range(2)]
        ssv = sb("ssv", [128, NTC * 4], F32)
        ssvs = sb("ssvs", [128, NTC], F32)
        rtv = sb("rtv", [128, NTC], F32)
        rstdv = sb("rstdv", [128, NTC], F32)
        wsTs = [sb(f"wsTs{i}", [128, NTC, 128], BF16) for i in range(2)]
        tmp_sz = [sb(f"tsz{i}", [128, TB], F32) for i in range(2)]
        tmp_m = [sb(f"tm{i}", [128, TB], F32) for i in range(2)]
        tmp_c = tmp_m
        tmp_a = [sb(f"ta{i}", [128, TB], F32) for i in range(2)]
        xcb = [sb(f"xc{i}", [128, TB + 2], F32) for i in range(2)]
        halo = sb("halo", [128, 2, KI, 2], F32)
        banks = [es.enter_context(nc.psum_tensor(f"bank{i}", [128, 512], F32)) for i in range(8)]

        state = {"bank": 0, "tile": 0, "slot": 0, "sq": 0, "sqb": 0, "nload": 0}

        def next_bank():
            b = state["bank"]
            state["bank"] = (b + 1) % 8
            return b

        def tbs(tb):
            return slice(tb * TB, (tb + 1) * TB)

        def load_wtile(extra_reads=()):
            s = state["slot"]
            state["slot"] = (s + 1) % NSLOT
            idx = state["tile"]
            state["tile"] += 1
            extra = tuple(extra_reads)
            first = state["nload"] < NSLOT
            if idx == 0 and not first:
                extra = extra + (("x", 0, 0),)
            if first:
                extra = extra + ((("w", s - 1),) if state["nload"] > 0 else (("x", 0, 0),))
            state["nload"] += 1
            P.add("pool", lambda e, s=s, idx=idx: e.dma_start(out=wslot[s][:, :], in_=wt_d[idx]),
                  reads=extra, writes=(("w", s),), dsem=("w", s))
            if state["nload"] == 1:
                input_x(0, 1, extra_reads=(("w", s),))
            if state["nload"] == NSLOT:
                load_big_consts()
            return s

        yo = v_sb[:, :, :].bitcast(F32)

        def input_x(h, tb, extra_reads=()):
            P.add("sp", lambda e, tb=tb, h=h: e.dma_start(out=x_sb[:, :, tbs(tb)], in_=x_d[:, h, tb, :, :]),
                  reads=tuple(extra_reads), writes=tuple(("x", k, tb) for k in range(KD)), dsem=("ld", tb))

        def output_y(h, tb, kh):
            k0 = kh * 4
            P.add("sp", lambda e, tb=tb, h=h, k0=k0: e.dma_start(out=out_d[:, h, tb, k0:k0 + 4, :], in_=yo[:, k0:k0 + 4, tbs(tb)]),
                  reads=tuple(("v", k, 2 * tb + i) for k in range(k0, k0 + 4) for i in range(2)), dsem=("st", tb, kh))

        P.add("sp", lambda e: e.dma_start(out=cst[:, 0:C_WS], in_=cst_d[:, 0:C_WS]), writes=(("cst",),), dsem=("cst",))
        input_x(0, 0)
        P.add("dve", lambda e: e.memset(acc[0][:, 0:128], 1.0), writes=(("acc", 0),))
        P.add("dve", lambda e: e.tensor_copy(out=ones_f[:, :].bitcast(F32R), in_=acc[0][:, 0:128]), reads=(("acc", 0),), writes=(("ones",),))
        P.add("dve", lambda e: e.memset(ones_bf[:, :], 1.0), writes=(("onesb",),))
        P.add("dve", lambda e: e.memset(eps_t[:, :], EPS), writes=(("eps",),))
        P.add("act", lambda e: e.activation(out=warm_t[:, :], in_=eps_t[:, :], func=AF.Square), reads=(("eps",),), writes=(("warm",),))
        P.add("dve", lambda e: e.memset(halo[:, :, :, :], 0.0),
              writes=tuple(("halo", lb, j) for lb in range(2) for j in range(KI)))

        def load_big_consts():
            P.add("sp", lambda e: e.dma_start(out=cst[:, C_WS:NCST], in_=cst_d[:, C_WS:NCST]),
                  reads=(("w", NSLOT - 1),), writes=(("cstb",),), dsem=("cstb",))
            for la in range(2):
                if 2 * la in layers:
                    ws_ap = cst[:, C_WS + la * 1024:C_WS + (la + 1) * 1024].rearrange("p (g t) -> p g t", g=8)
                    mk_ap = cst[:, C_MK:C_MK + 128].unsqueeze(1).broadcast_to([128, 8, 128])
                    P.add("dve", lambda e, ws_ap=ws_ap, mk_ap=mk_ap: e.tensor_tensor(out=ws_ap, in0=ws_ap, in1=mk_ap, op=ALU.mult),
                          reads=(("cst",), ("cstb",)), writes=(("wsTm", la),))

        def norm_acc(tb, k, from_yo=False):
            src = (lambda: yo[:, k, tbs(tb)]) if from_yo else (lambda: x_sb[:, k, tbs(tb)])
            rd = (("v", k, 2 * tb), ("v", k, 2 * tb + 1)) if from_yo else (("x", k, tb),)
            if k == 0:
                P.add("act", lambda e, tb=tb, src=src: e.activation(out=acc[tb][:, :].bitcast(F32R), in_=src(), func=AF.Square),
                      reads=rd, writes=(("acc", tb),))
                return
            i = state["sq"] % 3
            state["sq"] += 1
            P.add("act", lambda e, i=i, src=src: e.activation(out=sq[i][:, :], in_=src(), func=AF.Square),
                  reads=rd, writes=(("sq", i),))
            oap = (lambda tb=tb: acc[tb][:, :].bitcast(F32R))
            P.add("dve", lambda e, i=i, tb=tb, oap=oap: e.tensor_tensor(out=oap(), in0=acc[tb][:, :], in1=sq[i][:, :], op=ALU.add),
                  reads=(("acc", tb), ("sq", i)), writes=(("acc", tb),))

        def norm_fin(tb, gcol, to_x, pe_acc=False):
            b = next_bank()
            if pe_acc:
                for k in range(KD):
                    i = state["sqb"] % 4
                    state["sqb"] += 1
                    P.add("act", lambda e, i=i, k=k, tb=tb: e.activation(out=sqb[i][:, :], in_=x_sb[:, k, tbs(tb)], func=AF.Square),
                          reads=(("x", k, tb),), writes=(("sqb", i),))
                    P.add("pe", lambda e, b=b, i=i, k=k: e.matmul(banks[b][:, :], lhsT=ones_bf[:, :], rhs=sqb[i][:, :],
                                                                  start=(k == 0), stop=(k == KD - 1)),
                          reads=(("sqb", i), ("onesb",)), writes=(("bank", b),))
            else:
                P.add("pe", lambda e, b=b, tb=tb: e.matmul(banks[b][:, :], lhsT=ones_f[:, :].bitcast(F32R), rhs=acc[tb][:, :].bitcast(F32R), start=True, stop=True),
                      reads=(("acc", tb), ("ones",)), writes=(("bank", b),))
            P.add("act", lambda e, b=b, tb=tb: e.activation(out=rt[tb][:, :], in_=banks[b][:, :], func=AF.Sqrt,
                                                            scale=1.0 / D, bias=eps_t[:, 0:1]),
                  reads=(("bank", b), ("eps",)), writes=(("rstd", tb),))
            P.add("dve", lambda e, tb=tb: e.reciprocal(out=rstd[tb][:, :], in_=rt[tb][:, :]),
                  reads=(("rstd", tb),), writes=(("rstd", tb),))
            for k in range(KD):
                if to_x:
                    out_fn = lambda k=k, tb=tb: yo[:, k, tbs(tb)]
                    in_fn = out_fn
                    wr = (("v", k, 2 * tb), ("v", k, 2 * tb + 1))
                    rd = wr
                else:
                    out_fn = lambda k=k, tb=tb: hT[:, k, tbs(tb)]
                    in_fn = lambda k=k, tb=tb: x_sb[:, k, tbs(tb)]
                    wr = (("hT", k, tb),)
                    rd = (("x", k, tb),)
                P.add("dve", lambda e, k=k, tb=tb, out_fn=out_fn, in_fn=in_fn: e.scalar_tensor_tensor(
                    out=out_fn(), in0=in_fn(), scalar=cst[:, gcol + k:gcol + k + 1],
                    in1=rstd[tb][:, :], op0=ALU.mult, op1=ALU.mult),
                    reads=rd + (("rstd", tb), ("cst",)), writes=wr)
                if to_x and k % 4 == 3:
                    output_y(state["h"], tb, k // 4)

        def out_proj(hook, last=False):
            slots = []
            for tb in range(NTB):
                for dp in range(4):
                    if tb == 0:
                        slots.append(load_wtile())
                    s = slots[dp]
                    W = wslot[s][:, :].rearrange("p (d k c) -> p d k c", d=2, k=KI)
                    for dd in range(2):
                        dmc = 2 * dp + dd
                        b = next_bank()
                        for k in range(KI):
                            P.add("pe", lambda e, b=b, W=W, dd=dd, k=k, tb=tb: e.matmul(
                                banks[b][:, :], lhsT=W[:, dd, k, :], rhs=y_sb[:, k, tbs(tb)],
                                start=(k == 0), stop=(k == KI - 1)),
                                reads=(("w", s), ("y", k, tb)), writes=(("bank", b),))
                        if last:
                            P.add("dve", lambda e, b=b, dmc=dmc, tb=tb: e.tensor_tensor(
                                out=yo[:, dmc, tbs(tb)], in0=banks[b][:, :], in1=x_sb[:, dmc, tbs(tb)], op=ALU.add),
                                reads=(("bank", b), ("x", dmc, tb)), writes=(("v", dmc, 2 * tb), ("v", dmc, 2 * tb + 1)))
                        else:
                            P.add("dve", lambda e, b=b, dmc=dmc, tb=tb: e.tensor_tensor(
                                out=x_sb[:, dmc, tbs(tb)], in0=banks[b][:, :], in1=x_sb[:, dmc, tbs(tb)], op=ALU.add),
                                reads=(("bank", b), ("x", dmc, tb)), writes=(("x", dmc, tb),))
                        norm_acc(tb, dmc, from_yo=last)
                    if tb == 1 and dp == 0 and hook is not None:
                        hook()

        def layer_a(la, pending):
            slots = []
            for half in range(2):
                for nb in range(4):
                    if half == 0:
                        slots.append(load_wtile())
                    s = slots[nb]
                    W = wslot[s][:, :].rearrange("p (k c) -> p k c", k=KD)
                    for tc in range(half * 4, half * 4 + 4):
                        b = next_bank()
                        for k in range(KD):
                            P.add("pe", lambda e, b=b, W=W, k=k, tc=tc: e.matmul(
                                banks[b][:, :], lhsT=hT[:, k, tc * 128:(tc + 1) * 128], rhs=W[:, k, :],
                                start=(k == 0), stop=(k == KD - 1)),
                                reads=(("w", s), ("hT", k, tc // 4)), writes=(("bank", b),))
                        col = tc * 4 + nb
                        ji = col % 2
                        P.add("act", lambda e, b=b, col=col, ji=ji: e.activation(out=junk[ji][:, :], in_=banks[b][:, :], func=AF.Square,
                                                                                 accum_out=ssv[:, col:col + 1]),
                              reads=(("bank", b),), writes=(("ssv", col), ("junk", ji)))
                        P.add("act", lambda e, b=b, tc=tc, nb=nb: e.activation(out=v_sb[:, tc, nb * 512:(nb + 1) * 512], in_=banks[b][:, :], func=AF.Copy),
                              reads=(("bank", b),), writes=(("v", tc, nb),))
                    if half == 0 and nb in pending:
                        pending[nb]()
            P.add("dve", lambda e: e.tensor_reduce(out=ssvs[:, :], in_=ssv[:, :].rearrange("p (a c) -> p a c", c=4),
                                                   axis=AX.X, op=ALU.add),
                  reads=tuple(("ssv", c) for c in range(NTC * 4)), writes=(("ssvs",),))
            P.add("act", lambda e: e.activation(out=rtv[:, :], in_=ssvs[:, :], func=AF.Sqrt, scale=1.0 / DI, bias=eps_t[:, 0:1]),
                  reads=(("ssvs",), ("eps",)), writes=(("rtv",),))
            P.add("dve", lambda e: e.reciprocal(out=rstdv[:, :], in_=rtv[:, :]),
                  reads=(("rtv",),), writes=(("rstdv",),))
            unit = 0
            for jp in range(8):
                s = load_wtile()
                W = wslot[s][:, :].rearrange("p (a m k c) -> p a m k c", a=2, m=2, k=KD)
                g = jp
                wb = wsTs[g % 2]
                ws_g = cst[:, C_WS + la * 1024 + g * 128:C_WS + la * 1024 + (g + 1) * 128]
                P.add("dve", lambda e, wb=wb, ws_g=ws_g: e.tensor_tensor(
                    out=wb[:, :, :], in0=ws_g.unsqueeze(1).broadcast_to([128, NTC, 128]),
                    in1=rstdv[:, :].unsqueeze(2).broadcast_to([128, NTC, 128]), op=ALU.mult),
                    reads=(("wsTm", la), ("rstdv",)), writes=(("wsTs", g % 2),))
                bb_g = cst[:, C_BB + la * 1024 + g * 128:C_BB + la * 1024 + (g + 1) * 128]
                for jj in range(2):
                    j = 2 * jp + jj
                    gcol = C_GV + la * 16 + j
                    for tb in range(NTB):
                        bu, bz, bm = next_bank(), next_bank(), next_bank()
                        ts = unit % 2
                        unit += 1
                        for (bk, m) in ((bz, 1), (bu, 0)):
                            for k in range(KD):
                                P.add("pe", lambda e, bk=bk, W=W, jj=jj, m=m, k=k, tb=tb: e.matmul(
                                    banks[bk][:, :], lhsT=W[:, jj, m, k, :], rhs=hT[:, k, tbs(tb)],
                                    start=(k == 0), stop=(k == KD - 1)),
                                    reads=(("w", s), ("hT", k, tb)), writes=(("bank", bk),))
                        for q in range(4):
                            tc = tb * 4 + q
                            P.add("pe", lambda e, bm=bm, q=q, tc=tc, j=j, wb=wb: e.matmul(
                                banks[bm][:, q * 128:(q + 1) * 128], lhsT=v_sb[:, tc, j * 128:(j + 1) * 128],
                                rhs=wb[:, tc, :], start=True, stop=True),
                                reads=(("v", tc, j // 4), ("wsTs", g % 2)), writes=(("bank", bm),))
                        P.add("act", lambda e, bz=bz, ts=ts: e.activation(out=tmp_sz[ts][:, :], in_=banks[bz][:, :], func=AF.Silu),
                              reads=(("bank", bz),), writes=(("tsz", ts),))
                        P.add("dve", lambda e, bu=bu, ts=ts: e.tensor_tensor(
                            out=tmp_sz[ts][:, :], in0=banks[bu][:, :], in1=tmp_sz[ts][:, :], op=ALU.mult),
                            reads=(("bank", bu), ("tsz", ts)), writes=(("tsz", ts),))
                        P.add("dve", lambda e, bm=bm, ts=ts, gcol=gcol, bb_g=bb_g: e.scalar_tensor_tensor(
                            out=tmp_m[ts][:, :].rearrange("p (a c) -> p a c", a=4),
                            in0=banks[bm][:, :].rearrange("p (a c) -> p a c", a=4),
                            scalar=cst[:, gcol:gcol + 1],
                            in1=bb_g.unsqueeze(1).broadcast_to([128, 4, 128]),
                            op0=ALU.mult, op1=ALU.add),
                            reads=(("bank", bm), ("cst",), ("cstb",)), writes=(("tm", ts),))
                        P.add("dve", lambda e, ts=ts, j=j, tb=tb: e.tensor_tensor(
                            out=y_sb[:, j, tbs(tb)], in0=tmp_sz[ts][:, :], in1=tmp_m[ts][:, :], op=ALU.mult),
                            reads=(("tsz", ts), ("tm", ts)), writes=(("y", j, tb),))

        def layer_b(lb, pending):
            ucount = [0]

            def b_unit(j, tb, s):
                W = wslot[s][:, :].rearrange("p (m k c) -> p m k c", m=4, k=KD)
                wc = [C_WC + lb * 48 + tap * 16 + j for tap in range(3)]
                bC, bX, bZ, bB = next_bank(), next_bank(), next_bank(), next_bank()
                ts = ucount[0] % 2
                ucount[0] += 1
                for (bk, m) in ((bC, 1), (bX, 2), (bZ, 3), (bB, 0)):
                    for k in range(KD):
                        P.add("pe", lambda e, bk=bk, W=W, m=m, k=k, tb=tb: e.matmul(
                            banks[bk][:, :], lhsT=W[:, m, k, :], rhs=hT[:, k, tbs(tb)],
                            start=(k == 0), stop=(k == KD - 1)),
                            reads=(("w", s), ("hT", k, tb)), writes=(("bank", bk),))
                P.add("act", lambda e, j=j, ts=ts: e.activation(out=xcb[ts][:, 0:2], in_=halo[:, lb, j, :], func=AF.Copy),
                      reads=(("halo", lb, j),), writes=(("xch", ts),))
                P.add("act", lambda e, bC=bC, ts=ts: e.activation(out=tmp_c[ts][:, :], in_=banks[bC][:, :], func=AF.Copy),
                      reads=(("bank", bC),), writes=(("tm", ts),))
                P.add("dve", lambda e, bX=bX, ts=ts: e.tensor_tensor(
                    out=xcb[ts][:, 2:TB + 2], in0=banks[bX][:, :], in1=tmp_c[ts][:, :], op=ALU.mult),
                    reads=(("bank", bX), ("tm", ts)), writes=(("xc", ts),))
                P.add("act", lambda e, j=j, ts=ts: e.activation(out=halo[:, lb, j, :], in_=xcb[ts][:, TB:TB + 2], func=AF.Copy),
                      reads=(("xc", ts),), writes=(("halo", lb, j),))
                P.add("act", lambda e, ts=ts, c=wc[2]: e.activation(
                    out=tmp_a[ts][:, :], in_=xcb[ts][:, 2:TB + 2], func=AF.Copy, scale=cst[:, c:c + 1]),
                    reads=(("xc", ts), ("cst",)), writes=(("ta", ts),))
                P.add("act", lambda e, bZ=bZ, ts=ts: e.activation(out=tmp_sz[ts][:, :], in_=banks[bZ][:, :], func=AF.Silu),
                      reads=(("bank", bZ),), writes=(("tsz", ts),))
                P.add("dve", lambda e, ts=ts, c=wc[1]: e.scalar_tensor_tensor(
                    out=tmp_a[ts][:, :], in0=xcb[ts][:, 1:TB + 1], scalar=cst[:, c:c + 1], in1=tmp_a[ts][:, :],
                    op0=ALU.mult, op1=ALU.add),
                    reads=(("xc", ts), ("xch", ts), ("ta", ts), ("cst",)), writes=(("ta", ts),))
                P.add("dve", lambda e, ts=ts, c=wc[0]: e.scalar_tensor_tensor(
                    out=tmp_c[ts][:, :], in0=xcb[ts][:, 0:TB], scalar=cst[:, c:c + 1], in1=tmp_a[ts][:, :],
                    op0=ALU.mult, op1=ALU.add),
                    reads=(("xc", ts), ("xch", ts), ("ta", ts), ("cst",)), writes=(("tm", ts),))
                P.add("dve", lambda e, bB=bB, ts=ts: e.tensor_tensor(
                    out=tmp_sz[ts][:, :], in0=banks[bB][:, :], in1=tmp_sz[ts][:, :], op=ALU.mult),
                    reads=(("bank", bB), ("tsz", ts)), writes=(("tsz", ts),))
                P.add("dve", lambda e, ts=ts, j=j, tb=tb: e.tensor_tensor(
                    out=y_sb[:, j, tbs(tb)], in0=tmp_sz[ts][:, :], in1=tmp_c[ts][:, :], op=ALU.mult),
                    reads=(("tsz", ts), ("tm", ts)), writes=(("y", j, tb),))

            slots = []
            for j in range(4):
                slots.append(load_wtile())
                b_unit(j, 0, slots[j])
                if j in pending:
                    pending[j]()
            for j in range(4):
                b_unit(j, 1, slots[j])
            for j in range(4, KI):
                s = load_wtile()
                for tb in range(NTB):
                    b_unit(j, tb, s)

        nl = len(layers)
        assert layers[-1] % 2 == 1, 'final-norm staging reuses v_sb: last layer must be a short-conv layer'
        def acc_and_fin_tb1(l):
            for k in range(KD):
                norm_acc(1, k)
            norm_fin(1, C_GN + l * 8, to_x=False)

        def fin_prev_tb1(h):
            state["h"] = h - 1
            norm_fin(1, C_GF, to_x=True)
            state["h"] = h

        for h in range(NPASS):
            state["tile"] = 0
            state["h"] = h
            l0 = layers[0]
            norm_fin(0, C_GN + l0 * 8, to_x=False, pe_acc=True)
            for li, l in enumerate(layers):
                if li == 0 and h == 0:
                    pending = {1: (lambda: [norm_acc(1, k) for k in range(KD)]),
                               2: (lambda l=l: norm_fin(1, C_GN + l * 8, to_x=False))}
                elif li == 0:
                    pending = {0: (lambda h=h: fin_prev_tb1(h)),
                               1: (lambda l=l: norm_fin(1, C_GN + l * 8, to_x=False, pe_acc=True))}
                else:
                    pending = {0: (lambda l=l: norm_fin(1, C_GN + l * 8, to_x=False))}
                if l % 2 == 0:
                    layer_a(l // 2, pending)
                else:
                    layer_b(l // 2, pending)
                if li + 1 < nl:
                    hook = (lambda ln=layers[li + 1]: norm_fin(0, C_GN + ln * 8, to_x=False))
                    out_proj(hook)
                else:
                    def hook(h=h):
                        if h + 1 < NPASS:
                            input_x(h + 1, 0)
                        norm_fin(0, C_GF, to_x=True)
                    out_proj(hook, last=True)
                    if h + 1 < NPASS:
                        input_x(h + 1, 1)
        state["h"] = NPASS - 1
        norm_fin(1, C_GF, to_x=True)
        P.add("sp", None, writes=tuple(("v", k, i) for k in range(NTC) for i in range(4)))

        P.resolve()
        keys = P.sem_keys()
        sems = {}
        for i, key in enumerate(keys):
            sems[key] = es.enter_context(nc.semaphore(f"s{i}"))
        block = es.enter_context(nc.Block())

        @block.tensor
        def _(e):
            P.emit("pe", e, sems)

        @block.scalar
        def _(e):
            P.emit("act", e, sems)

        @block.vector
        def _(e):
            P.emit("dve", e, sems)

        @block.gpsimd
        def _(e):
            P.emit("pool", e, sems)

        @block.sync
        def _(e):
            P.emit("sp", e, sems)
    return nc


def _pkc(w):
    k = w.shape[0] // 128
    return np.ascontiguousarray(w.reshape(k, 128, w.shape[1]).transpose(1, 0, 2))


def _o_tiles(w_out):
    wr = _pkc(w_out)
    t = wr.reshape(128, KI, 4, 2, 128)
    t = t.transpose(2, 0, 3, 1, 4)
    return np.ascontiguousarray(t).reshape(4, 128, 4096)


def _a_tiles(w_in, w_out):
    wr = _pkc(w_in)
    v = wr[:, :, DI:2 * DI].reshape(128, KD, 4, 512).transpose(2, 0, 1, 3)
    a1 = np.ascontiguousarray(v).reshape(4, 128, 4096)
    u = wr[:, :, 0:DI].reshape(128, KD, 8, 2, 128)
    z = wr[:, :, 2 * DI:3 * DI].reshape(128, KD, 8, 2, 128)
    uz = np.stack([u, z], axis=0)
    a2 = np.ascontiguousarray(uz.transpose(3, 1, 4, 0, 2, 5)).reshape(8, 128, 4096)
    return np.concatenate([a1, a2, _o_tiles(w_out)], axis=0)


def _b_tiles(w_in, w_out):
    wr = _pkc(w_in)
    t = wr.reshape(128, KD, 4, KI, 128)
    b1 = np.ascontiguousarray(t.transpose(3, 0, 2, 1, 4)).reshape(KI, 128, 4096)
    return np.concatenate([b1, _o_tiles(w_out)], axis=0)


def _consts(norm_g, final_g, a_v_norm_g, a_w_s, a_b_s, b_w_conv):
    c = np.zeros((128, NCST), np.float32)
    c[:, C_GN:C_GN + 32] = norm_g.reshape(4, KD, 128).transpose(2, 0, 1).reshape(128, 32)
    c[:, C_GF:C_GF + 8] = final_g.reshape(KD, 128).T
    c[:, C_GV:C_GV + 32] = a_v_norm_g.reshape(2, KI, 128).transpose(2, 0, 1).reshape(128, 32)
    c[:, C_WC:C_WC + 96] = b_w_conv.reshape(2, 3, KI, 128).transpose(3, 0, 1, 2).reshape(128, 96)
    c[:, C_ID:C_ID + 128] = np.eye(128, dtype=np.float32)
    c[:, C_MK:C_MK + 128] = np.triu(np.ones((128, 128), np.float32))
    c[:, C_WS:C_WS + 2048] = a_w_s.transpose(3, 0, 1, 2).reshape(128, 2048)
    c[:, C_BB:C_BB + 2048] = np.broadcast_to(a_b_s.reshape(1, 2048), (128, 2048))
    return c


LAUNCH_GROUPS = [[0, 1, 2, 3]]
_prog_cache = {}


def kernel(x, norm_g, final_g, a_w_in, a_v_norm_g, a_w_s, a_b_s, a_w_out, b_w_in, b_w_conv, b_w_out):
    f = lambda a: np.asarray(a, dtype=np.float32)
    x = f(x)
    cst = _consts(f(norm_g), f(final_g), f(a_v_norm_g), f(a_w_s), f(a_b_s), f(b_w_conv))
    tiles = {}
    for l in range(4):
        if l % 2 == 0:
            tiles[l] = _a_tiles(f(a_w_in)[l // 2], f(a_w_out)[l // 2])
        else:
            tiles[l] = _b_tiles(f(b_w_in)[l // 2], f(b_w_out)[l // 2])
    cur = [np.ascontiguousarray(x[c].reshape(NPASS, NTB, TB, KD, 128).transpose(4, 0, 1, 3, 2)) for c in range(N_CORES)]
    for gi, group in enumerate(LAUNCH_GROUPS):
        last = gi == len(LAUNCH_GROUPS) - 1
        key = (tuple(group), last)
        if key not in _prog_cache:
            _prog_cache[key] = build_program(group, final_norm=last)
        nc = _prog_cache[key]
        wt = np.concatenate([tiles[l] for l in group], axis=0)
        in_maps = [{"x": cur[c], "wt": wt, "cst": cst} for c in range(N_CORES)]
        res = run_bass_kernel_spmd(nc, in_maps, core_ids=list(range(N_CORES)))
        cur = [np.asarray(res.results[c]["out"], dtype=np.float32) for c in range(N_CORES)]
    o = np.stack(cur, axis=0)
    return np.ascontiguousarray(o.transpose(0, 2, 3, 5, 4, 1)).reshape(N_CORES, SEQ, D)
```
